# Optimizing a Trainium2 kernel written in Bass

```python
import math
import jax, jax.numpy as jnp
from jax import lax
import numpy as np

D_MODEL = 1024
BATCH = 4
SEQ = 8192
DEPTH = 1

HEAD_DIM = 64
SB_HEADS = 8
SB_WIDTH = SB_HEADS * HEAD_DIM
DIFF_HEADS = 4
DIFF_QK_WIDTH = DIFF_HEADS * 2 * HEAD_DIM
DIFF_V_DIM = 2 * HEAD_DIM
DIFF_WIDTH = DIFF_HEADS * DIFF_V_DIM
BLOCK_Q = 128
NORM_EPS = 1e-6
SPLIT_SIZES = (SB_WIDTH, SB_WIDTH, SB_WIDTH, SB_WIDTH,
               DIFF_QK_WIDTH, DIFF_QK_WIDTH, DIFF_WIDTH, DIFF_WIDTH,
               D_MODEL, D_MODEL)
IN_COLS = sum(SPLIT_SIZES)

kernel_name = "stickbreak_diffattn_gated_hybrid"


def rms_norm(x, g):
    x32 = x.astype(jnp.float32)
    y = x32 * lax.rsqrt(jnp.mean(x32 * x32, axis=-1, keepdims=True) + NORM_EPS)
    return (y * g.astype(jnp.float32)).astype(x.dtype)


def split_cols(proj):
    parts, off = [], 0
    for n in SPLIT_SIZES:
        parts.append(proj[..., off:off + n])
        off += n
    return parts


def alibi_slopes(n_heads):
    return jnp.asarray(np.array([2.0 ** (-8.0 * (h + 1) / n_heads) for h in range(n_heads)], dtype=np.float32))


def to_blocks(t):
    B, S = t.shape[:2]
    t = t.reshape((B, S // BLOCK_Q, BLOCK_Q) + t.shape[2:])
    return jnp.moveaxis(t, 1, 0)


def from_blocks(t):
    t = jnp.moveaxis(t, 0, 1)
    return t.reshape((t.shape[0], t.shape[1] * t.shape[2]) + t.shape[3:])


def stick_breaking_attention(q, k, v):
    S, D = q.shape[1], q.shape[-1]
    scale = D ** -0.5
    kpos = jnp.arange(S)
    starts = jnp.arange(S // BLOCK_Q) * BLOCK_Q

    def block(args):
        q_blk, start = args
        z = jnp.einsum('bqhd,bkhd->bhqk', q_blk, k, preferred_element_type=jnp.float32) * scale
        qpos = start + jnp.arange(BLOCK_Q)
        mask = kpos[None, :] < qpos[:, None]
        neg_log_1m_beta = jnp.where(mask, jax.nn.softplus(z), 0.0)
        suffix = lax.cumsum(neg_log_1m_beta, axis=3, reverse=True) - neg_log_1m_beta
        log_a = jax.nn.log_sigmoid(z) - suffix
        a = jnp.where(mask, jnp.exp(log_a), 0.0)
        return jnp.einsum('bhqk,bkhd->bqhd', a.astype(v.dtype), v)

    out = lax.map(block, (to_blocks(q), starts))
    return from_blocks(out)


def differential_attention(q, k, v, slopes, lam):
    S, D = q.shape[1], q.shape[-1]
    scale = D ** -0.5
    kpos = jnp.arange(S)
    starts = jnp.arange(S // BLOCK_Q) * BLOCK_Q

    def block(args):
        q_blk, start = args
        s = jnp.einsum('bqhmd,bkhmd->bhmqk', q_blk, k, preferred_element_type=jnp.float32) * scale
        qpos = start + jnp.arange(BLOCK_Q)
        dist = (qpos[:, None] - kpos[None, :]).astype(jnp.float32)
        bias = -slopes[:, None, None, None] * dist
        s = jnp.where(dist >= 0, s + bias, -jnp.inf)
        p = jax.nn.softmax(s, axis=-1)
        attn = p[:, :, 0] - lam * p[:, :, 1]
        return jnp.einsum('bhqk,bkhe->bqhe', attn.astype(v.dtype), v)

    out = lax.map(block, (to_blocks(q), starts))
    return from_blocks(out)


def setup_inputs(seed: int = 0) -> dict:
    key = jax.random.key(seed)
    ks = jax.random.split(key, 14)
    f32 = jnp.float32
    nrm = lambda k, shape, s: jax.random.normal(k, shape, f32) * s
    return {
        "x": nrm(ks[0], (BATCH, SEQ, D_MODEL), 1.0),
        "pre_norm_g": 1.0 + nrm(ks[1], (DEPTH, D_MODEL), 0.02),
        "w_in": nrm(ks[2], (DEPTH, D_MODEL, IN_COLS), D_MODEL ** -0.5),
        "b_gate": nrm(ks[3], (DEPTH, 2 * D_MODEL), 0.02),
        "lambda_q1": nrm(ks[4], (DEPTH, HEAD_DIM), 0.1),
        "lambda_k1": nrm(ks[5], (DEPTH, HEAD_DIM), 0.1),
        "lambda_q2": nrm(ks[6], (DEPTH, HEAD_DIM), 0.1),
        "lambda_k2": nrm(ks[7], (DEPTH, HEAD_DIM), 0.1),
        "subln_g": 1.0 + nrm(ks[8], (DEPTH, DIFF_V_DIM), 0.02),
        "w_o_sb": nrm(ks[9], (DEPTH, SB_WIDTH, D_MODEL), SB_WIDTH ** -0.5),
        "w_o_diff": nrm(ks[10], (DEPTH, DIFF_WIDTH, D_MODEL), DIFF_WIDTH ** -0.5),
        "w_out": nrm(ks[11], (DEPTH, D_MODEL, D_MODEL), D_MODEL ** -0.5),
        "post_norm_g": 1.0 + nrm(ks[12], (DEPTH, D_MODEL), 0.02),
    }


def reference(x, pre_norm_g, w_in, b_gate, lambda_q1, lambda_k1, lambda_q2, lambda_k2,
              subln_g, w_o_sb, w_o_diff, w_out, post_norm_g):
    B, S, _ = x.shape
    slopes = alibi_slopes(DIFF_HEADS)
    for layer in range(DEPTH):
        h = rms_norm(x, pre_norm_g[layer])
        proj = jnp.einsum('bsd,de->bse', h, w_in[layer])
        (sb_q, sb_k, sb_v, sb_z, df_q, df_k, df_v, df_z, g_sb, g_df) = split_cols(proj)

        sb_out = stick_breaking_attention(
            sb_q.reshape(B, S, SB_HEADS, HEAD_DIM),
            sb_k.reshape(B, S, SB_HEADS, HEAD_DIM),
            sb_v.reshape(B, S, SB_HEADS, HEAD_DIM)).reshape(B, S, SB_WIDTH)
        y_sb = jnp.einsum('bse,ed->bsd', sb_out * jax.nn.silu(sb_z), w_o_sb[layer])

        lambda_init = 0.8 - 0.6 * math.exp(-0.3 * layer)
        lam = (jnp.exp(jnp.sum(lambda_q1[layer] * lambda_k1[layer]).astype(jnp.float32))
               - jnp.exp(jnp.sum(lambda_q2[layer] * lambda_k2[layer]).astype(jnp.float32))
               + lambda_init)
        df_out = differential_attention(
            df_q.reshape(B, S, DIFF_HEADS, 2, HEAD_DIM),
            df_k.reshape(B, S, DIFF_HEADS, 2, HEAD_DIM),
            df_v.reshape(B, S, DIFF_HEADS, DIFF_V_DIM), slopes, lam)
        df_out = (rms_norm(df_out, subln_g[layer]) * (1.0 - lambda_init)).reshape(B, S, DIFF_WIDTH)
        y_df = jnp.einsum('bse,ed->bsd', df_out * jax.nn.silu(df_z), w_o_diff[layer])

        gates = jax.nn.sigmoid(jnp.concatenate([g_sb, g_df], axis=-1) + b_gate[layer])
        merged = gates[..., :D_MODEL] * y_sb + gates[..., D_MODEL:] * y_df
        out = jnp.einsum('bsd,de->bse', merged, w_out[layer])
        x = x + rms_norm(out, post_norm_g[layer])
    return x
```

```python
import math
from contextlib import ExitStack

import numpy as np
import ml_dtypes

import concourse.bass as bass
import concourse.mybir as mybir
from concourse.bass_utils import run_bass_kernel_spmd

F32 = mybir.dt.float32
BF16 = mybir.dt.bfloat16
AF = mybir.ActivationFunctionType
ALU = mybir.AluOpType

D = 1024
HD = 64
NCORES = 8
EPS = 1e-6
LAMBDA_INIT = 0.8 - 0.6 * math.exp(-0.3 * 0)
SLOPES = [2.0 ** (-8.0 * (h + 1) / 4) for h in range(4)]
STRIP_W = 897
NDMASEM = 24


class Buf:
    __slots__ = ("name", "w", "r", "war")

    def __init__(self, name=""):
        self.name = name
        self.w = []
        self.r = []
        self.war = []


class T:
    __slots__ = ("t", "b")

    def __init__(self, t, name=""):
        self.t = t
        self.b = Buf(name)


class Op:
    __slots__ = ("eng", "fn", "deps", "dma", "sig", "sem", "val", "idx", "seq", "vc")

    def __init__(self, eng, fn, dma):
        self.eng = eng
        self.fn = fn
        self.dma = dma
        self.deps = []
        self.sig = False
        self.sem = None
        self.val = 0
        self.idx = 0


def _b(x):
    return x.b if isinstance(x, T) else x


class Sched:
    ENGS = ("pe", "act", "dve", "pool", "sp")

    def __init__(self, nc, es):
        self.nc = nc
        self.ops = {e: [] for e in self.ENGS}
        self.esem = {e: es.enter_context(nc.semaphore("s_" + e)) for e in self.ENGS}
        self.ecnt = {e: 0 for e in self.ENGS}
        self.dsem = [es.enter_context(nc.semaphore("s_dma%d" % i)) for i in range(NDMASEM)]
        self.dcnt = [0] * NDMASEM
        self.dnext = 0
        self.waited = {e: {} for e in self.ENGS}
        self.all_dma = []
        self.seq = 0
        self.evc = {e: {} for e in self.ENGS}

    def add(self, eng, fn, reads=(), writes=(), pwrites=(), dma=False):
        op = Op(eng, fn, dma)
        self.seq += 1
        op.seq = self.seq
        deps = []
        for x in reads:
            b = _b(x)
            deps.extend(b.w)
        for x in writes:
            b = _b(x)
            deps.extend(b.w)
            deps.extend(b.r)
            deps.extend(b.war)
        for x in pwrites:
            b = _b(x)
            if b.r:
                b.war = list(b.r)
                b.r = []
                b.w = []
            deps.extend(b.war)
        op.deps = deps
        for x in reads:
            b = _b(x)
            if not dma:
                b.r = [o for o in b.r if o.dma or o.eng != eng]
            b.r.append(op)
        for x in writes:
            b = _b(x)
            b.w = [op]
            b.r = []
            b.war = []
        for x in pwrites:
            b = _b(x)
            if not dma:
                b.w = [o for o in b.w if o.dma or o.eng != eng]
            b.w.append(op)
        self.ops[eng].append(op)
        if dma:
            self.all_dma.append(op)
        return op

    def dma(self, out, in_, reads=(), writes=(), pwrites=(), q="sp"):
        def fn(e):
            return e.dma_start(out=out, in_=in_)
        return self.add(q, fn, reads, writes, pwrites, dma=True)

    def mm(self, out, lhsT, rhs, start, stop, reads=(), writes=(), pwrites=(), skip=False):
        def fn(e):
            if skip:
                return e.matmul(out, lhsT, rhs, start=start, stop=stop, skip_group_check=True)
            return e.matmul(out, lhsT, rhs, start=start, stop=stop)
        return self.add("pe", fn, reads, writes, pwrites)

    def tr(self, out, in_, ident, reads=(), writes=(), pwrites=()):
        def fn(e):
            return e.transpose(out, in_, ident)
        return self.add("pe", fn, reads, writes, pwrites)

    def act(self, out, in_, func, bias=None, scale=None, accum_out=None,
            reads=(), writes=(), pwrites=()):
        kw = {}
        if bias is not None:
            kw["bias"] = bias
        if scale is not None:
            kw["scale"] = scale
        if accum_out is not None:
            kw["accum_out"] = accum_out

        def fn(e):
            return e.activation(out=out, in_=in_, func=func, **kw)
        return self.add("act", fn, reads, writes, pwrites)

    def tt(self, eng, out, in0, in1, op, reads=(), writes=(), pwrites=()):
        def fn(e):
            return e.tensor_tensor(out=out, in0=in0, in1=in1, op=op)
        return self.add(eng, fn, reads, writes, pwrites)

    def ts(self, eng, out, in0, s1, s2, op0, op1=None, reads=(), writes=(), pwrites=()):
        def fn(e):
            if op1 is None:
                return e.tensor_scalar(out=out, in0=in0, scalar1=s1, scalar2=None, op0=op0)
            return e.tensor_scalar(out=out, in0=in0, scalar1=s1, scalar2=s2, op0=op0, op1=op1)
        return self.add(eng, fn, reads, writes, pwrites)

    def stt(self, out, in0, scalar, in1, op0, op1, reads=(), writes=(), pwrites=()):
        def fn(e):
            return e.scalar_tensor_tensor(out=out, in0=in0, scalar=scalar, in1=in1,
                                          op0=op0, op1=op1)
        return self.add("dve", fn, reads, writes, pwrites)

    def copy(self, eng, out, in_, reads=(), writes=(), pwrites=()):
        if eng == "act":
            return self.act(out, in_, AF.Copy, reads=reads, writes=writes, pwrites=pwrites)

        def fn(e):
            return e.tensor_copy(out=out, in_=in_)
        return self.add(eng, fn, reads, writes, pwrites)

    def recip(self, out, in_, reads=(), writes=(), pwrites=()):
        def fn(e):
            return e.reciprocal(out=out, in_=in_)
        return self.add("dve", fn, reads, writes, pwrites)

    def memset(self, eng, ap, val, writes=(), pwrites=()):
        def fn(e):
            return e.memset(ap, val)
        return self.add(eng, fn, (), writes, pwrites)

    def emit(self, final=False):
        nc = self.nc
        for e in self.ENGS:
            for op in self.ops[e]:
                for d in op.deps:
                    if d.eng == "pe" and op.eng == "pe" and not d.dma:
                        continue
                    d.sig = True
        for e in self.ENGS:
            lst = [o for o in self.ops[e] if not o.dma]
            if lst:
                lst[-1].sig = True
        for op in self.all_dma:
            op.sig = True
        for e in self.ENGS:
            for op in self.ops[e]:
                if not op.sig:
                    continue
                if op.dma:
                    k = self.dnext % NDMASEM
                    self.dnext += 1
                    self.dcnt[k] += 16
                    op.sem = self.dsem[k]
                    op.val = self.dcnt[k]
                    op.idx = ("d", k)
                else:
                    self.ecnt[e] += 1
                    op.sem = self.esem[e]
                    op.val = self.ecnt[e]
                    op.idx = ("e", e)
        prev_on = {}
        for op in sorted(self.all_dma, key=lambda o: o.seq):
            if op.idx in prev_on:
                op.deps.append(prev_on[op.idx])
            prev_on[op.idx] = op
        targets = []
        for e in self.ENGS:
            if self.ecnt[e] > 0:
                targets.append((("e", e), self.esem[e], self.ecnt[e]))
        for k in range(NDMASEM):
            if self.dcnt[k] > 0:
                targets.append((("d", k), self.dsem[k], self.dcnt[k]))
        ops = self.ops
        waited = self.waited
        allops = sorted((o for e in self.ENGS for o in self.ops[e]), key=lambda o: o.seq)
        evc = self.evc
        for op in allops:
            if op.dma:
                vc = {}
            else:
                vc = evc[op.eng]
            for d in op.deps:
                for k, v in d.vc.items():
                    if vc.get(k, 0) < v:
                        vc[k] = v
            if op.sig:
                vc[op.idx] = op.val
            if op.dma:
                op.vc = vc
            else:
                op.vc = dict(vc)

        def run(ename, eng):
            wd = waited[ename]
            for op in ops[ename]:
                need = {}
                for d in op.deps:
                    if d.eng == "pe" and ename == "pe" and not d.dma:
                        continue
                    if wd.get(d.idx, 0) >= d.val:
                        continue
                    if d.idx not in need or need[d.idx].val < d.val:
                        need[d.idx] = d
                rem = list(need.values())
                keep = []
                for d in rem:
                    cov = False
                    for o in rem:
                        if o is not d and o.vc.get(d.idx, 0) >= d.val:
                            cov = True
                            break
                    if not cov:
                        keep.append(d)
                for d in keep:
                    for k, v in d.vc.items():
                        if wd.get(k, 0) < v:
                            wd[k] = v
                wl = [(d.sem, d.val) for d in keep]
                for sem, val in wl[:-1]:
                    eng.wait_ge(sem, val)
                inst = op.fn(eng)
                if wl:
                    inst._wait_ge(wl[-1][0], wl[-1][1])
                if op.sig:
                    inst.then_inc(op.sem, 16 if op.dma else 1)
            for key, sem, val in targets:
                if wd.get(key, 0) < val:
                    eng.wait_ge(sem, val)
                    wd[key] = val

        with nc.Block() as block:
            @block.tensor
            def _(eng):
                run("pe", eng)

            @block.scalar
            def _(eng):
                run("act", eng)

            @block.vector
            def _(eng):
                run("dve", eng)

            @block.gpsimd
            def _(eng):
                run("pool", eng)

            @block.sync
            def _(eng):
                run("sp", eng)
        self.ops = {e: [] for e in self.ENGS}
        self.all_dma = []


def tiles_for(parity, nslot):
    res = []
    for j in range(nslot):
        lo, hi = 2 * j, 2 * j + 1
        a, b = (lo, hi) if j % 2 == 0 else (hi, lo)
        res.append(a if parity == 0 else b)
    return res


def bias_index(nslot):
    idx = {}
    n = 0
    for h in range(4):
        for j in range(nslot):
            for kb in range(8 * j + 8):
                idx[(h, j, kb)] = n
                n += 1
    return idx, n


def build_program(SEQ, phases="A1,A2,B,C", debug=False):
    SQ = SEQ // 2
    NT = SEQ // 512
    NSLOT = SQ // 512
    NKB = SEQ // 128
    bidx, NBIAS = bias_index(NSLOT)
    phases = phases.split(",")

    nc = bass.Bass("TRN2", target_bir_lowering=False)
    dk = "ExternalOutput" if debug else "Internal"

    def din(name, shape, dt):
        return nc.dram_tensor(name, shape, dt, kind="ExternalInput").ap()

    def dscr(name, shape, dt):
        if debug:
            return nc.dram_tensor(name, shape, dt, kind="ExternalOutput").ap()
        return nc.dram_tensor(name, shape, dt).ap()

    xk = din("xk", [SEQ, D], F32)
    xq = din("xq", [SQ, D], F32)
    w_in = din("w_in", [D, 6144], F32)
    w_o_sb = din("w_o_sb", [512, D], F32)
    w_o_df = din("w_o_df", [512, D], F32)
    w_out = din("w_out", [D, D], F32)
    pre_g = din("pre_g", [128, D], F32)
    post_g = din("post_g", [128, D], F32)
    bgate = din("bgate", [128, 16], F32)
    subln = din("subln", [128, 1], F32)
    lam_in = din("lam_in", [128, 4 * HD], F32)
    ident_d = din("ident", [128, 128], BF16)
    negtri_d = din("negtri", [128, 256], BF16)
    ones_d = din("ones", [128, 128], BF16)
    onesf_d = din("onesf", [128, 128], F32)
    strips_d = din("strips", [128, 4 * STRIP_W], BF16)
    bias_d = din("biastab", [128, NBIAS], F32)
    out_d = nc.dram_tensor("out", [SQ, D], F32, kind="ExternalOutput").ap()

    KT = dscr("KT", [1024, SEQ], BF16)
    Vg = dscr("Vg", [8, 128, NKB, 128], BF16)
    QTs = dscr("QTs", [NSLOT, 128, 8, 512], BF16)
    ZTs = dscr("ZTs", [NSLOT, 128, 8, 512], BF16)
    GTs = dscr("GTs", [NSLOT, 128, 16, 512], BF16)
    OTs = dscr("OTs", [NSLOT, 128, 8, 512], BF16)

    uid = [0]
    with ExitStack() as es0:
        S = Sched(nc, es0)

        def phase_A(which):
            with ExitStack() as es:
                def sb(name, shape, dt):
                    uid[0] += 1
                    nm = "sb%d_%s" % (uid[0], name)
                    return T(es.enter_context(nc.sbuf_tensor(nm, shape, dt)), nm)

                def ps(name, shape, dt):
                    uid[0] += 1
                    nm = "ps%d_%s" % (uid[0], name)
                    return T(es.enter_context(nc.psum_tensor(nm, shape, dt)), nm)

                ident = sb("ident", [128, 128], BF16)
                S.dma(ident.t[:], ident_d, writes=[ident])
                gpre = sb("gpre", [128, D], F32)
                S.dma(gpre.t[:], pre_g, writes=[gpre])
                xt = [sb("xt%d" % i, [128, D], F32) for i in range(3)]
                junk = sb("junk", [128, D], BF16)
                ss = [sb("ss%d" % i, [128, 1], F32) for i in range(3)]
                lnv = [sb("lnv%d" % i, [128, 1], F32) for i in range(3)]
                rstd = [sb("rstd%d" % i, [128, 1], F32) for i in range(3)]
                hb = [sb("hb%d" % i, [128, D], BF16) for i in range(2)]
                hT = [sb("hT%d" % i, [128, 8, 512], BF16) for i in range(2)]
                stg = [sb("stg%d" % i, [128, 8, 512], F32) for i in range(2)]
                tp = [ps("tp%d" % i, [128, 1024], BF16) for i in range(2)]
                acc = [ps("pacc%d" % i, [128, 512], F32) for i in range(5)]
                cnt = {"stg": 0, "blk": 0, "acc": 0, "ev": 0}

                def load_w(dst, dcol, src, scol, nch):
                    st = stg[cnt["stg"] % 2]
                    cnt["stg"] += 1
                    S.dma(st.t[:, 0:nch, :],
                          src[:, scol:scol + 512].rearrange("(c p) n -> p c n", p=128),
                          writes=[st])
                    S.copy("dve" if cnt["stg"] % 2 == 1 else "pool", dst.t[:, :, dcol:dcol + 512],
                           st.t[:, 0:nch, :], reads=[st], pwrites=[dst])

                if which == 1:
                    xsrc, ntile = xk, NT
                    wk = sb("wk", [128, 8, 1024], BF16)
                    wv = sb("wv", [128, 8, 1024], BF16)
                    load_w(wk, 0, w_in, 512, 8)
                    load_w(wk, 512, w_in, 2560, 8)
                    load_w(wv, 0, w_in, 1024, 8)
                    load_w(wv, 512, w_in, 3072, 8)
                    kst = [sb("kst%d" % i, [128, 8, 512], BF16) for i in range(2)]
                    vst = [sb("vst%d" % i, [128, 8, 4, 128], BF16) for i in range(2)]
                else:
                    xsrc, ntile = xq, NSLOT
                    wq = sb("wq", [128, 8, 2048], BF16)
                    wg = sb("wg", [128, 8, 2048], BF16)
                    load_w(wq, 0, w_in, 0, 8)
                    load_w(wq, 512, w_in, 2048, 8)
                    load_w(wq, 1024, w_in, 1536, 8)
                    load_w(wq, 1536, w_in, 3584, 8)
                    for i in range(4):
                        load_w(wg, 512 * i, w_in, 4096 + 512 * i, 8)
                    bg = sb("bg", [128, 16], F32)
                    S.dma(bg.t[:], bgate, writes=[bg])
                    qst = [sb("qst%d" % i, [128, 8, 512], BF16) for i in range(2)]

                nst = {}

                def normA(tt, sub):
                    i = cnt["blk"]
                    cnt["blk"] += 1
                    nst[(tt, sub)] = i
                    x_ = xt[i % 3]
                    r0 = tt * 512 + sub * 128
                    S.dma(x_.t[:], xsrc[r0:r0 + 128, :], writes=[x_])
                    s_, l_, r_ = ss[i % 3], lnv[i % 3], rstd[i % 3]
                    S.act(junk.t[:], x_.t[:], AF.Square, accum_out=s_.t[:],
                          reads=[x_], writes=[junk, s_])
                    S.act(l_.t[:], s_.t[:], AF.Ln, bias=EPS, scale=1.0 / D,
                          reads=[s_], writes=[l_])
                    S.act(r_.t[:], l_.t[:], AF.Exp, scale=-0.5, reads=[l_], writes=[r_])
                    hb_ = hb[i % 2]
                    S.stt(hb_.t[:], x_.t[:], r_.t[:], gpre.t[:], ALU.mult, ALU.mult,
                          reads=[x_, r_, gpre], writes=[hb_])

                def normB(tt, sub):
                    i = nst[(tt, sub)]
                    h = hT[tt % 2]
                    hb_ = hb[i % 2]
                    tp_ = tp[i % 2]
                    for c in range(8):
                        S.tr(tp_.t[:, c * 128:(c + 1) * 128], hb_.t[:, c * 128:(c + 1) * 128],
                             ident.t[:], reads=[hb_, ident], pwrites=[tp_])
                    S.copy("dve" if sub % 2 == 0 else "act",
                           h.t[:, :, sub * 128:(sub + 1) * 128],
                           tp_.t[:].rearrange("p (c n) -> p c n", c=8),
                           reads=[tp_], pwrites=[h])

                def evac(eng, out, in_, func, bias, scale, reads, pwrites):
                    if func is None and eng == "dve":
                        if scale is None:
                            S.copy("dve", out, in_, reads=reads, pwrites=pwrites)
                        else:
                            S.ts("dve", out, in_, scale, None, ALU.mult, reads=reads,
                                 pwrites=pwrites)
                    else:
                        S.act(out, in_, func if func is not None else AF.Copy, bias=bias,
                              scale=scale, reads=reads, pwrites=pwrites)

                def proj1(tt):
                    h = hT[tt % 2]
                    k_ = kst[tt % 2]
                    v_ = vst[tt % 2]
                    for kc in range(8):
                        a_ = acc[cnt["acc"] % 5]
                        cnt["acc"] += 1
                        for c in range(8):
                            S.mm(a_.t[:], wk.t[:, c, kc * 128:(kc + 1) * 128], h.t[:, c, :],
                                 c == 0, c == 7, reads=[wk, h], pwrites=[a_])
                        eng = "dve" if cnt["ev"] % 2 == 0 else "act"
                        cnt["ev"] += 1
                        evac(eng, k_.t[:, kc, :], a_.t[:], None, None, None, [a_], [k_])
                        yield
                    S.dma(KT[:, tt * 512:(tt + 1) * 512].rearrange("(c p) n -> p c n", p=128),
                          k_.t[:], reads=[k_])
                    for sub in range(4):
                        for slab in range(2):
                            a_ = acc[cnt["acc"] % 5]
                            cnt["acc"] += 1
                            for c in range(8):
                                S.mm(a_.t[:], h.t[:, c, sub * 128:(sub + 1) * 128],
                                     wv.t[:, c, slab * 512:(slab + 1) * 512],
                                     c == 0, c == 7, reads=[wv, h], pwrites=[a_])
                            eng = "dve" if cnt["ev"] % 2 == 0 else "act"
                            cnt["ev"] += 1
                            evac(eng, v_.t[:, slab * 4:(slab + 1) * 4, sub, :],
                                 a_.t[:].rearrange("p (g n) -> p g n", g=4),
                                 None, None, None, [a_], [v_])
                            yield
                    S.dma(Vg[:, :, tt * 4:(tt + 1) * 4, :].rearrange("g p s n -> p g s n"),
                          v_.t[:], reads=[v_])

                def proj2(tt):
                    h = hT[tt % 2]
                    for grp in range(4):
                        q_ = qst[(tt * 4 + grp) % 2]
                        for oc8 in range(8):
                            oc = grp * 8 + oc8
                            a_ = acc[cnt["acc"] % 5]
                            cnt["acc"] += 1
                            wsrc, wc = (wq, oc) if oc < 16 else (wg, oc - 16)
                            for c in range(8):
                                S.mm(a_.t[:], wsrc.t[:, c, wc * 128:(wc + 1) * 128], h.t[:, c, :],
                                     c == 0, c == 7, reads=[wsrc, h], pwrites=[a_])
                            if grp == 0:
                                eng = "dve" if oc8 % 2 == 0 else "act"
                                evac(eng, q_.t[:, oc8, :], a_.t[:], None, None, 0.125, [a_], [q_])
                            elif grp == 1:
                                evac("act", q_.t[:, oc8, :], a_.t[:], AF.Silu, None, None,
                                     [a_], [q_])
                            else:
                                gc = oc - 16
                                evac("act", q_.t[:, oc8, :], a_.t[:], AF.Sigmoid,
                                     bg.t[:, gc:gc + 1], None, [a_, bg], [q_])
                            yield
                        dst = (QTs[tt], ZTs[tt], GTs[tt, :, 0:8, :], GTs[tt, :, 8:16, :])[grp]
                        S.dma(dst, q_.t[:], reads=[q_])

                proj = proj1 if which == 1 else proj2
                G = 16 if which == 1 else 32
                marks = {1 + (G * s_) // 4: s_ for s_ in range(4)}
                for sub in range(4):
                    normA(0, sub)
                    normB(0, sub)
                for tt in range(ntile):
                    gi = 0
                    nxt = tt + 1 < ntile
                    for _ in proj(tt):
                        gi += 1
                        if nxt and gi in marks:
                            sub = marks[gi]
                            normA(tt + 1, sub)
                            if sub > 0:
                                normB(tt + 1, sub - 1)
                    if nxt:
                        normB(tt + 1, 3)
                S.emit()

        if "A1" in phases:
            phase_A(1)
        if "A2" in phases:
            phase_A(2)

        def phase_B2():
            from collections import deque
            with ExitStack() as es:
                def sb(name, shape, dt):
                    uid[0] += 1
                    nm = "sb%d_%s" % (uid[0], name)
                    return T(es.enter_context(nc.sbuf_tensor(nm, shape, dt)), nm)

                def ps(name, shape, dt):
                    uid[0] += 1
                    nm = "ps%d_%s" % (uid[0], name)
                    return T(es.enter_context(nc.psum_tensor(nm, shape, dt)), nm)

                negtri = sb("negtri", [128, 256], BF16)
                S.dma(negtri.t[:], negtri_d, writes=[negtri])
                ones = sb("ones", [128, 128], BF16)
                S.dma(ones.t[:], ones_d, writes=[ones])
                onesf = sb("onesf", [128, 128], F32)
                S.dma(onesf.t[:], onesf_d, writes=[onesf])
                strips = sb("strips", [128, 4 * STRIP_W], BF16)
                S.dma(strips.t[:], strips_d, writes=[strips])
                biast = sb("biast", [128, NBIAS], F32)
                S.dma(biast.t[:], bias_d, writes=[biast])
                sublc = sb("sublc", [128, 1], F32)
                S.dma(sublc.t[:], subln, writes=[sublc])
                lamv = sb("lamv", [128, 4 * HD], F32)
                S.dma(lamv.t[:], lam_in, writes=[lamv])
                lprod = sb("lprod", [128, 2 * HD], F32)
                lsum = sb("lsum", [128, 2], F32)
                lexp = sb("lexp", [128, 2], F32)
                neglam = sb("neglam", [128, 1], F32)
                gcol = sb("gcol", [128, 1], F32)
                S.tt("dve", lprod.t[:, 0:HD], lamv.t[:, 0:HD], lamv.t[:, HD:2 * HD], ALU.mult,
                     reads=[lamv], pwrites=[lprod])
                S.tt("dve", lprod.t[:, HD:2 * HD], lamv.t[:, 2 * HD:3 * HD], lamv.t[:, 3 * HD:4 * HD],
                     ALU.mult, reads=[lamv], pwrites=[lprod])

                def red(e):
                    return e.tensor_reduce(out=lsum.t[:], in_=lprod.t[:].rearrange("p (a b) -> p a b", a=2),
                                           axis=mybir.AxisListType.X, op=ALU.add)
                S.add("dve", red, reads=[lprod], writes=[lsum])
                S.act(lexp.t[:], lsum.t[:], AF.Exp, reads=[lsum], writes=[lexp])
                S.stt(neglam.t[:], lexp.t[:, 1:2], -LAMBDA_INIT, lexp.t[:, 0:1], ALU.add, ALU.subtract,
                      reads=[lexp], writes=[neglam])
                S.ts("dve", gcol.t[:], sublc.t[:], 1.0 - LAMBDA_INIT, None, ALU.mult,
                     reads=[sublc], writes=[gcol])

                NCH = 8
                CH = NKB // NCH
                kTs = sb("kTs", [128, SEQ], BF16)
                vvs = sb("vvs", [128, NKB, 128], BF16)
                kTd = sb("kTd", [128, SEQ], BF16)
                vvd = sb("vvd", [128, NKB, 128], BF16)
                kTs_b = [Buf() for _ in range(NCH)]
                vvs_b = [Buf() for _ in range(NCH)]
                kTd_b = [Buf() for _ in range(NCH)]
                vvd_b = [Buf() for _ in range(NCH)]
                qs_ = [sb("qs%d" % i, [128, 512], BF16) for i in range(3)]
                zs_ = [sb("zs%d" % i, [128, 512], BF16) for i in range(3)]
                qd_ = [sb("qd%d" % i, [128, 512], BF16) for i in range(3)]
                zd_ = [sb("zd%d" % i, [128, 512], BF16) for i in range(3)]
                os_ = [sb("os%d" % i, [128, 512], BF16) for i in range(2)]
                od_ = [sb("od%d" % i, [128, 512], BF16) for i in range(2)]
                NW = 4
                e_t = [sb("e%d" % i, [128, 512], F32) for i in range(NW)]
                sp_t = [sb("sp%d" % i, [128, 512], BF16) for i in range(NW)]
                acc_t = [sb("ac%d" % i, [128, 512], BF16) for i in range(NW)]
                a_t = [sb("a%d" % i, [128, 512], BF16) for i in range(NW)]
                NP = 3
                p_t = [sb("p%d" % i, [128, 512], BF16) for i in range(NP)]
                NZS = 3
                zsb = [ps("zsb%d" % i, [128, 512], F32) for i in range(NZS)]
                NZD = 1
                zdf = [ps("zdf%d" % i, [128, 512], F32) for i in range(NZD)]
                subps = ps("subps", [128, 512], F32)
                obs = ps("obs", [128, 512], F32)
                obs_b = [Buf(), Buf()]
                obd = ps("obd", [128, 512], F32)
                lbd = ps("lbd", [128, 512], F32)
                uraw = [[sb("uraw%d_%d" % (i, m), [128, 512], F32) for m in range(2)] for i in range(2)]
                lraw = [[sb("lraw%d_%d" % (i, m), [128, 512], F32) for m in range(2)] for i in range(2)]
                r_t = [sb("r%d" % i, [128, 512], F32) for i in range(2)]
                u_t = [sb("u%d" % i, [128, 512], F32) for i in range(3)]
                cnt = {"zs": 0, "zd": 0, "w": 0, "p": 0}
                bg = deque()
                for t_ in sp_t + a_t + p_t:
                    S.memset("pool", t_.t[:], 0.0, writes=[t_])

                def ucols(kb, nkb):
                    i = kb - (nkb - 4)
                    return slice(128 * i, 512) if i > 0 else slice(0, 512)

                def strip(which_set, which_sb, sub, nonstrict):
                    k = which_set * 2 + which_sb
                    off = k * STRIP_W + 384 - 128 * sub + (1 if nonstrict else 0)
                    return strips.t[:, off:off + 512]

                def load_pair(g):
                    gd = 4 + g
                    for c in range(NCH - 1, -1, -1):
                        ks = slice(c * CH * 128, (c + 1) * CH * 128)
                        S.dma(kTs.t[:, ks], KT[g * 128:(g + 1) * 128, ks], writes=[kTs_b[c]], q="sp")
                        S.dma(vvs.t[:, c * CH:(c + 1) * CH, :], Vg[g, :, c * CH:(c + 1) * CH, :],
                              writes=[vvs_b[c]], q="sp")
                        S.dma(kTd.t[:, ks], KT[gd * 128:(gd + 1) * 128, ks], writes=[kTd_b[c]], q="sp")
                        S.dma(vvd.t[:, c * CH:(c + 1) * CH, :], Vg[gd, :, c * CH:(c + 1) * CH, :],
                              writes=[vvd_b[c]], q="sp")

                def load_slot(g, j):
                    gd = 4 + g
                    i = j % 3
                    cs = slice(j * 512, (j + 1) * 512)
                    S.dma(qs_[i].t[:], QTs[j, :, g, :], writes=[qs_[i]])
                    S.dma(zs_[i].t[:], ZTs[j, :, g, :], writes=[zs_[i]])
                    S.dma(qd_[i].t[:], QTs[j, :, gd, :], writes=[qd_[i]])
                    S.dma(zd_[i].t[:], ZTs[j, :, gd, :], writes=[zd_[i]])

                def run_pair(g):
                    gd = 4 + g
                    units = []
                    order = list(range(NSLOT - 1, -1, -1))
                    for j in order:
                        nkb = 8 * j + 8
                        for half in range(2):
                            for n in range(nkb):
                                units.append((j, half, n, nkb))
                    NU = len(units)
                    st = [None] * NU
                    load_slot(g, order[0])
                    d_cut = 120.0 / SLOPES[g]

                    def kbmin(j):
                        return max(0, int((1024 * j - d_cut) // 128))

                    def s0(u):
                        j, half, n, nkb = units[u]
                        if n == 0 and half == 0:
                            oi = order.index(j)
                            if oi + 1 < len(order):
                                load_slot(g, order[oi + 1])
                        kb = nkb - 1 - n
                        sl = j % 2
                        prt = slice(half * 64, (half + 1) * 64)
                        z = zsb[cnt["zs"] % NZS]
                        cnt["zs"] += 1
                        w = cnt["w"] % NW
                        cnt["w"] += 1
                        st[u] = {"z": z, "w": w}
                        S.mm(z.t[:], kTs.t[prt, kb * 128:(kb + 1) * 128], qs_[j % 3].t[prt, :],
                             True, True, reads=[kTs_b[kb // CH], qs_[j % 3]], pwrites=[z])

                    def s1(u):
                        j, half, n, nkb = units[u]
                        cs = ucols(nkb - 1 - n, nkb)
                        z, w = st[u]["z"], st[u]["w"]
                        S.act(e_t[w].t[:, cs], z.t[:, cs], AF.Exp, reads=[z], writes=[e_t[w]])

                    def s2(u):
                        j, half, n, nkb = units[u]
                        kb = nkb - 1 - n
                        z, w = st[u]["z"], st[u]["w"]
                        cs = ucols(kb, nkb)
                        S.act(sp_t[w].t[:, cs], e_t[w].t[:, cs], AF.Ln, bias=1.0, scale=1.0,
                              reads=[e_t[w]], writes=[sp_t[w]])
                        if kb >= nkb - 8:
                            rel = kb - (nkb - 8)
                            m = strip(j % 2, rel // 4, rel % 4, False)
                            S.tt("dve", sp_t[w].t[:], sp_t[w].t[:], m, ALU.mult,
                                 reads=[strips, sp_t[w]], writes=[sp_t[w]])

                    def s3(u):
                        j, half, n, nkb = units[u]
                        z, w = st[u]["z"], st[u]["w"]
                        S.mm(z.t[:], negtri.t[:, 0:128], sp_t[w].t[:], False, n == 0,
                             reads=[negtri, sp_t[w]], pwrites=[z], skip=True)
                        if n > 0:
                            pw = st[u - 1]["w"]
                            S.mm(z.t[:], negtri.t[:, 128:256], acc_t[pw].t[:], False, True,
                                 reads=[negtri, acc_t[pw]], pwrites=[z], skip=True)
                        if n + 1 < nkb:
                            if n == 0:
                                S.copy("dve", acc_t[w].t[:], sp_t[w].t[:], reads=[sp_t[w]],
                                       writes=[acc_t[w]])
                            else:
                                pw = st[u - 1]["w"]
                                S.tt("dve", acc_t[w].t[:], acc_t[pw].t[:], sp_t[w].t[:], ALU.add,
                                     reads=[acc_t[pw], sp_t[w]], writes=[acc_t[w]])

                    def s4(u):
                        j, half, n, nkb = units[u]
                        kb = nkb - 1 - n
                        z, w = st[u]["z"], st[u]["w"]
                        cs = ucols(kb, nkb)
                        S.act(a_t[w].t[:, cs], z.t[:, cs], AF.Exp, reads=[z], writes=[a_t[w]])
                        if kb >= nkb - 8:
                            rel = kb - (nkb - 8)
                            m = strip(j % 2, rel // 4, rel % 4, False)
                            S.tt("dve", a_t[w].t[:], a_t[w].t[:], m, ALU.mult,
                                 reads=[strips, a_t[w]], writes=[a_t[w]])

                    def s5(u):
                        j, half, n, nkb = units[u]
                        kb = nkb - 1 - n
                        sl = j % 2
                        prt = slice(half * 64, (half + 1) * 64)
                        w = st[u]["w"]
                        S.mm(obs.t[prt, :], vvs.t[:, kb, half * 64:(half + 1) * 64], a_t[w].t[:],
                             n == 0, n == nkb - 1, reads=[vvs_b[kb // CH], a_t[w]], pwrites=[obs_b[half]])
                        if n == nkb - 1:
                            S.tt("dve", os_[sl].t[prt, :], obs.t[prt, :], zs_[j % 3].t[prt, :], ALU.mult,
                                 reads=[obs_b[half], zs_[j % 3]], pwrites=[os_[sl]])
                            if half == 1:
                                S.dma(OTs[j, :, g, :], os_[sl].t[:], reads=[os_[sl]])

                    def d0(u):
                        j, half, n, nkb = units[u]
                        if n < kbmin(j):
                            return
                        kb = n
                        sl = j % 2
                        prt = slice((1 - half) * 64, (2 - half) * 64)
                        zd = zdf[cnt["zd"] % NZD]
                        cnt["zd"] += 1
                        wp = cnt["p"] % NP
                        cnt["p"] += 1
                        st[u]["zd"] = zd
                        st[u]["wp"] = wp
                        S.mm(zd.t[:], kTd.t[prt, kb * 128:(kb + 1) * 128], qd_[j % 3].t[prt, :],
                             True, True, reads=[kTd_b[kb // CH], qd_[j % 3]], pwrites=[zd])

                    def d1(u):
                        j, half, n, nkb = units[u]
                        if n < kbmin(j):
                            return
                        kb = n
                        zd, wp = st[u]["zd"], st[u]["wp"]
                        bc = bidx[(g, j, kb)]
                        cs = ucols(kb, nkb)
                        S.act(p_t[wp].t[:, cs], zd.t[:, cs], AF.Exp, reads=[zd], writes=[p_t[wp]])
                        if kb >= nkb - 8:
                            rel = kb - (nkb - 8)
                            mk = strip(j % 2, rel // 4, rel % 4, True)
                            S.stt(p_t[wp].t[:], p_t[wp].t[:], biast.t[:, bc:bc + 1], mk, ALU.mult, ALU.mult,
                                  reads=[strips, biast, p_t[wp]], writes=[p_t[wp]])
                        else:
                            S.ts("dve", p_t[wp].t[:], p_t[wp].t[:], biast.t[:, bc:bc + 1], None, ALU.mult,
                                 reads=[biast, p_t[wp]], writes=[p_t[wp]])

                    def slot_epilogue(j):
                        sl = j % 2
                        ur, lr = uraw[sl], lraw[sl]
                        ops = []
                        for m in range(2):
                            for q4 in range(4):
                                cs = slice(q4 * 128, (q4 + 1) * 128)
                                ops.append(lambda m=m, cs=cs: S.recip(r_t[m].t[:, cs], lr[m].t[:, cs],
                                                                     reads=[lr[m]], pwrites=[r_t[m]]))
                            ops.append(lambda m=m: S.tt("dve", u_t[m].t[:], ur[m].t[:], r_t[m].t[:], ALU.mult,
                                                        reads=[ur[m], r_t[m]], writes=[u_t[m]]))
                        ops.append(lambda: S.stt(u_t[2].t[:], u_t[1].t[:], neglam.t[:], u_t[0].t[:],
                                                 ALU.mult, ALU.add,
                                                 reads=[u_t[0], u_t[1], neglam], writes=[u_t[2]]))

                        def tail():
                            S.act(r_t[0].t[:], u_t[2].t[:], AF.Square, reads=[u_t[2]], writes=[r_t[0]])
                            zq = subps
                            S.mm(zq.t[:], onesf.t[:], r_t[0].t[:], True, True, reads=[onesf, r_t[0]],
                                 pwrites=[zq])
                            S.act(r_t[1].t[:], zq.t[:], AF.Ln, bias=EPS, scale=1.0, reads=[zq],
                                  writes=[r_t[1]])
                            S.act(r_t[0].t[:], r_t[1].t[:], AF.Exp, scale=-0.5, reads=[r_t[1]],
                                  writes=[r_t[0]])
                            S.tt("dve", u_t[0].t[:], u_t[2].t[:], r_t[0].t[:], ALU.mult,
                                 reads=[u_t[2], r_t[0]], writes=[u_t[0]])
                            S.stt(od_[sl].t[:], u_t[0].t[:], gcol.t[:], zd_[j % 3].t[:], ALU.mult, ALU.mult,
                                  reads=[u_t[0], gcol, zd_[j % 3]], writes=[od_[sl]])
                            S.dma(OTs[j, :, gd, :], od_[sl].t[:], reads=[od_[sl]])
                        ops.append(tail)
                        bg.extend(ops)

                    def d2(u):
                        j, half, n, nkb = units[u]
                        if n < kbmin(j):
                            return
                        kb = n
                        sl = j % 2
                        wp = st[u]["wp"]
                        first = (n == kbmin(j))
                        S.mm(obd.t[:], vvd.t[:, kb, :], p_t[wp].t[:], first, n == nkb - 1,
                             reads=[vvd_b[kb // CH], p_t[wp]], pwrites=[obd])
                        S.mm(lbd.t[:], ones.t[:], p_t[wp].t[:], first, n == nkb - 1,
                             reads=[ones, p_t[wp]], pwrites=[lbd])
                        if n == nkb - 1:
                            S.copy("dve", uraw[sl][1 - half].t[:], obd.t[:], reads=[obd],
                                   writes=[uraw[sl][1 - half]])
                            S.copy("dve", lraw[sl][1 - half].t[:], lbd.t[:], reads=[lbd],
                                   writes=[lraw[sl][1 - half]])
                            if half == 1:
                                slot_epilogue(j)

                    sched = [(s5, 4), (s4, 3), (s3, 2), (d2, 2), (s1, 1), (d1, 1), (s2, 1), (s0, 0), (d0, 0)]
                    for step in range(NU + 5):
                        for fn, sk in sched:
                            u = step - sk
                            if 0 <= u < NU:
                                fn(u)
                        for _ in range(2):
                            if bg:
                                bg.popleft()()
                    while bg:
                        bg.popleft()()

                load_pair(0)
                for g in range(4):
                    run_pair(g)
                    if g + 1 < 4:
                        load_pair(g + 1)
                S.emit()

        if "B" in phases:
            phase_B2()

        def phase_C():
            with ExitStack() as es:
                def sb(name, shape, dt):
                    uid[0] += 1
                    nm = "sb%d_%s" % (uid[0], name)
                    return T(es.enter_context(nc.sbuf_tensor(nm, shape, dt)), nm)

                def ps(name, shape, dt):
                    uid[0] += 1
                    nm = "ps%d_%s" % (uid[0], name)
                    return T(es.enter_context(nc.psum_tensor(nm, shape, dt)), nm)

                stg = [sb("stgc%d" % i, [128, 8, 512], F32) for i in range(2)]
                wosb = sb("wosb", [128, 4, 1024], BF16)
                wodf = sb("wodf", [128, 4, 1024], BF16)
                wout = sb("wout", [128, 8, 1024], BF16)
                pg = sb("pg", [128, D], F32)
                S.dma(pg.t[:], post_g, writes=[pg])
                n = 0
                for dst, src, nch in ((wosb, w_o_sb, 4), (wodf, w_o_df, 4), (wout, w_out, 8)):
                    for half in range(2):
                        st = stg[n % 2]
                        n += 1
                        S.dma(st.t[:, 0:nch, :],
                              src[:, half * 512:(half + 1) * 512].rearrange("(c p) n -> p c n", p=128),
                              writes=[st])
                        S.copy("dve" if n % 2 == 1 else "pool", dst.t[:, :, half * 512:(half + 1) * 512],
                               st.t[:, 0:nch, :], reads=[st], pwrites=[dst])
                oTt = [sb("oTt%d" % i, [128, 8, 512], BF16) for i in range(2)]
                Gt = [sb("Gt%d" % i, [128, 16, 512], BF16) for i in range(2)]
                xr = [sb("xr%d" % i, [128, 4, D], F32) for i in range(2)]
                m1 = [sb("m1_%d" % i, [128, 512], F32) for i in range(2)]
                m2 = [sb("m2_%d" % i, [128, 512], F32) for i in range(2)]
                mg = [sb("mg%d" % i, [128, 8, 512], BF16) for i in range(2)]
                ot = [sb("ot%d" % i, [128, D], F32) for i in range(2)]
                t1 = [sb("t1_%d" % i, [128, D], F32) for i in range(2)]
                junk = sb("junkc", [128, 512], BF16)
                ssa = [sb("ssa%d" % i, [128, 2], F32) for i in range(2)]
                sst = [sb("sst%d" % i, [128, 1], F32) for i in range(2)]
                lnv = [sb("lnvc%d" % i, [128, 1], F32) for i in range(2)]
                rstd = [sb("rstdc%d" % i, [128, 1], F32) for i in range(2)]
                py = [ps("py%d" % i, [128, 512], F32) for i in range(4)]
                po = [ps("po%d" % i, [128, 512], F32) for i in range(4)]
                cnt = {"y": 0, "o": 0, "m": 0, "s": 0}

                def load(j):
                    i = j % 2
                    cs = slice(j * 512, (j + 1) * 512)
                    S.dma(oTt[i].t[:], OTs[j], writes=[oTt[i]])
                    S.dma(Gt[i].t[:], GTs[j], writes=[Gt[i]])
                    S.dma(xr[i].t[:], xq[cs, :].rearrange("(s p) n -> p s n", p=128), writes=[xr[i]])

                def tile(j):
                    i = j % 2
                    o_, g_, x_, mg_ = oTt[i], Gt[i], xr[i], mg[i]
                    for dc in range(8):
                        ya = py[cnt["y"] % 4]
                        yb = py[(cnt["y"] + 1) % 4]
                        cnt["y"] += 2
                        for ec in range(4):
                            S.mm(ya.t[:], wosb.t[:, ec, dc * 128:(dc + 1) * 128], o_.t[:, ec, :],
                                 ec == 0, ec == 3, reads=[wosb, o_], pwrites=[ya])
                        for ec in range(4):
                            S.mm(yb.t[:], wodf.t[:, ec, dc * 128:(dc + 1) * 128], o_.t[:, 4 + ec, :],
                                 ec == 0, ec == 3, reads=[wodf, o_], pwrites=[yb])
                        k = cnt["m"] % 2
                        cnt["m"] += 1
                        S.tt("dve", m1[k].t[:], ya.t[:], g_.t[:, dc, :], ALU.mult,
                             reads=[ya, g_], writes=[m1[k]])
                        S.tt("dve", m2[k].t[:], yb.t[:], g_.t[:, 8 + dc, :], ALU.mult,
                             reads=[yb, g_], writes=[m2[k]])
                        S.tt("pool", mg_.t[:, dc, :], m1[k].t[:], m2[k].t[:], ALU.add,
                             reads=[m1[k], m2[k]], pwrites=[mg_])
                    for sub in range(4):
                        k = cnt["s"] % 2
                        cnt["s"] += 1
                        pa = po[cnt["o"] % 4]
                        pb = po[(cnt["o"] + 1) % 4]
                        cnt["o"] += 2
                        for half, p_ in ((0, pa), (1, pb)):
                            for dc in range(8):
                                S.mm(p_.t[:], mg_.t[:, dc, sub * 128:(sub + 1) * 128],
                                     wout.t[:, dc, half * 512:(half + 1) * 512],
                                     dc == 0, dc == 7, reads=[mg_, wout], pwrites=[p_])
                        S.act(junk.t[:], pa.t[:], AF.Square, accum_out=ssa[k].t[:, 0:1],
                              reads=[pa], writes=[junk], pwrites=[ssa[k]])
                        S.act(junk.t[:], pb.t[:], AF.Square, accum_out=ssa[k].t[:, 1:2],
                              reads=[pb], writes=[junk], pwrites=[ssa[k]])
                        S.tt("dve", sst[k].t[:], ssa[k].t[:, 0:1], ssa[k].t[:, 1:2], ALU.add,
                             reads=[ssa[k]], writes=[sst[k]])
                        S.act(lnv[k].t[:], sst[k].t[:], AF.Ln, bias=EPS, scale=1.0 / D,
                              reads=[sst[k]], writes=[lnv[k]])
                        S.act(rstd[k].t[:], lnv[k].t[:], AF.Exp, scale=-0.5, reads=[lnv[k]],
                              writes=[rstd[k]])
                        for half, p_ in ((0, pa), (1, pb)):
                            hs = slice(half * 512, (half + 1) * 512)
                            S.stt(t1[k].t[:, hs], p_.t[:], rstd[k].t[:], pg.t[:, hs], ALU.mult, ALU.mult,
                                  reads=[p_, rstd[k], pg], pwrites=[t1[k]])
                        S.tt("pool", ot[k].t[:], t1[k].t[:], x_.t[:, sub, :], ALU.add,
                             reads=[t1[k], x_], writes=[ot[k]])
                        r0 = j * 512 + sub * 128
                        S.dma(out_d[r0:r0 + 128, :], ot[k].t[:], reads=[ot[k]])

                load(0)
                for j in range(NSLOT):
                    if j + 1 < NSLOT:
                        load(j + 1)
                    tile(j)
                S.emit()

        if "C" in phases:
            phase_C()
    return nc


def make_consts(SEQ, parity):
    SQ = SEQ // 2
    NSLOT = SQ // 512
    bf = ml_dtypes.bfloat16
    ident = np.eye(128, dtype=np.float32).astype(bf)
    jj = np.arange(128)[:, None]
    s_ = np.arange(128)[None, :]
    negtri = np.concatenate([-(jj >= s_).astype(np.float32), -np.ones((128, 128), np.float32)], axis=1).astype(bf)
    ones = np.ones((128, 128), np.float32).astype(bf)
    onesf = np.full((128, 128), 1.0 / 128, np.float32)
    xx = np.arange(STRIP_W)[None, :]
    ss = np.arange(128)[:, None]
    tri = ((xx - 384) > ss).astype(np.float32)
    zer = np.zeros((128, STRIP_W), np.float32)
    one = np.ones((128, STRIP_W), np.float32)
    lo_set = [tri, zer]
    hi_set = [one, tri]
    sets = [lo_set, hi_set] if parity == 0 else [hi_set, lo_set]
    strips = np.concatenate([sets[0][0], sets[0][1], sets[1][0], sets[1][1]], axis=1).astype(bf)
    bidx, nb = bias_index(NSLOT)
    tiles = tiles_for(parity, NSLOT)
    bias = np.zeros((128, nb), np.float32)
    sl = np.arange(128, dtype=np.float64)
    for (h, j, kb), c in bidx.items():
        tref = 512 * tiles[j] + 256
        bias[:, c] = np.exp(SLOPES[h] * (128 * kb + sl - tref)).astype(np.float32)
        if kb // 4 > tiles[j]:
            bias[:, c] = 0.0
    return dict(ident=ident, negtri=negtri, ones=ones, onesf=onesf, strips=strips, biastab=bias)


def make_in_maps(SEQ, x, pre_norm_g, w_in, b_gate, lambda_q1, lambda_k1, lambda_q2, lambda_k2,
                 subln_g, w_o_sb, w_o_diff, w_out, post_norm_g):
    f = lambda a: np.ascontiguousarray(np.asarray(a, dtype=np.float32))
    x = f(x)
    SQ = SEQ // 2
    NSLOT = SQ // 512
    shared = dict(
        w_in=f(w_in[0]), w_o_sb=f(w_o_sb[0]), w_o_df=f(w_o_diff[0]), w_out=f(w_out[0]),
        pre_g=f(np.broadcast_to(np.asarray(pre_norm_g[0])[None, :], (128, D))),
        post_g=f(np.broadcast_to(np.asarray(post_norm_g[0])[None, :], (128, D))),
        bgate=f(np.asarray(b_gate[0]).reshape(16, 128).T),
        subln=f(np.asarray(subln_g[0]).reshape(128, 1)),
        lam_in=f(np.broadcast_to(np.concatenate([np.asarray(lambda_q1[0]), np.asarray(lambda_k1[0]),
                                                 np.asarray(lambda_q2[0]), np.asarray(lambda_k2[0])])[None, :],
                                 (128, 4 * HD))),
    )
    consts = [make_consts(SEQ, 0), make_consts(SEQ, 1)]
    in_maps = []
    for c in range(NCORES):
        b, par = c // 2, c % 2
        tiles = tiles_for(par, NSLOT)
        xb = x[b]
        xq = np.concatenate([xb[t * 512:(t + 1) * 512] for t in tiles], axis=0)
        m = dict(shared)
        m.update(consts[par])
        m["xk"] = np.ascontiguousarray(xb)
        m["xq"] = np.ascontiguousarray(xq)
        in_maps.append(m)
    return in_maps


_NC_CACHE = {}


def kernel(x, pre_norm_g, w_in, b_gate, lambda_q1, lambda_k1, lambda_q2, lambda_k2,
           subln_g, w_o_sb, w_o_diff, w_out, post_norm_g):
    x = np.asarray(x)
    B, SEQ, _ = x.shape
    SQ = SEQ // 2
    NSLOT = SQ // 512
    in_maps = make_in_maps(SEQ, x, pre_norm_g, w_in, b_gate, lambda_q1, lambda_k1, lambda_q2,
                           lambda_k2, subln_g, w_o_sb, w_o_diff, w_out, post_norm_g)
    if SEQ not in _NC_CACHE:
        _NC_CACHE[SEQ] = build_program(SEQ)
    nc = _NC_CACHE[SEQ]
    res = run_bass_kernel_spmd(nc, in_maps, core_ids=list(range(NCORES)))
    out = np.empty((B, SEQ, D), np.float32)
    for c in range(NCORES):
        b, par = c // 2, c % 2
        tiles = tiles_for(par, NSLOT)
        o = np.asarray(res.results[c]["out"]).reshape(SQ, D)
        for j, t in enumerate(tiles):
            out[b, t * 512:(t + 1) * 512] = o[j * 512:(j + 1) * 512]
    return out
```

```python
import math
from contextlib import ExitStack

import numpy as np
import ml_dtypes

import concourse.bass as bass
import concourse.mybir as mybir
from concourse.bass_utils import run_bass_kernel_spmd

F32 = mybir.dt.float32
BF16 = mybir.dt.bfloat16
AF = mybir.ActivationFunctionType
ALU = mybir.AluOpType

D = 1024
HD = 64
NCORES = 8
EPS = 1e-6
LAMBDA_INIT = 0.8 - 0.6 * math.exp(-0.3 * 0)
SLOPES = [2.0 ** (-8.0 * (h + 1) / 4) for h in range(4)]
STRIP_W = 897
NDMASEM = 24


class Buf:
    __slots__ = ("name", "w", "r", "war")

    def __init__(self, name=""):
        self.name = name
        self.w = []
        self.r = []
        self.war = []


class T:
    __slots__ = ("t", "b")

    def __init__(self, t, name=""):
        self.t = t
        self.b = Buf(name)


class Op:
    __slots__ = ("eng", "fn", "deps", "dma", "sig", "sem", "val", "idx", "seq", "vc")

    def __init__(self, eng, fn, dma):
        self.eng = eng
        self.fn = fn
        self.dma = dma
        self.deps = []
        self.sig = False
        self.sem = None
        self.val = 0
        self.idx = 0


def _b(x):
    return x.b if isinstance(x, T) else x


class Sched:
    ENGS = ("pe", "act", "dve", "pool", "sp")

    def __init__(self, nc, es):
        self.nc = nc
        self.ops = {e: [] for e in self.ENGS}
        self.esem = {e: es.enter_context(nc.semaphore("s_" + e)) for e in self.ENGS}
        self.ecnt = {e: 0 for e in self.ENGS}
        self.dsem = [es.enter_context(nc.semaphore("s_dma%d" % i)) for i in range(NDMASEM)]
        self.dcnt = [0] * NDMASEM
        self.dnext = 0
        self.waited = {e: {} for e in self.ENGS}
        self.all_dma = []
        self.seq = 0
        self.evc = {e: {} for e in self.ENGS}

    def add(self, eng, fn, reads=(), writes=(), pwrites=(), dma=False):
        op = Op(eng, fn, dma)
        self.seq += 1
        op.seq = self.seq
        deps = []
        for x in reads:
            b = _b(x)
            deps.extend(b.w)
        for x in writes:
            b = _b(x)
            deps.extend(b.w)
            deps.extend(b.r)
            deps.extend(b.war)
        for x in pwrites:
            b = _b(x)
            if b.r:
                b.war = list(b.r)
                b.r = []
                b.w = []
            deps.extend(b.war)
        op.deps = deps
        for x in reads:
            b = _b(x)
            if not dma:
                b.r = [o for o in b.r if o.dma or o.eng != eng]
            b.r.append(op)
        for x in writes:
            b = _b(x)
            b.w = [op]
            b.r = []
            b.war = []
        for x in pwrites:
            b = _b(x)
            if not dma:
                b.w = [o for o in b.w if o.dma or o.eng != eng]
            b.w.append(op)
        self.ops[eng].append(op)
        if dma:
            self.all_dma.append(op)
        return op

    def dma(self, out, in_, reads=(), writes=(), pwrites=(), q="sp"):
        def fn(e):
            return e.dma_start(out=out, in_=in_)
        return self.add(q, fn, reads, writes, pwrites, dma=True)

    def mm(self, out, lhsT, rhs, start, stop, reads=(), writes=(), pwrites=(), skip=False):
        def fn(e):
            if skip:
                return e.matmul(out, lhsT, rhs, start=start, stop=stop, skip_group_check=True)
            return e.matmul(out, lhsT, rhs, start=start, stop=stop)
        return self.add("pe", fn, reads, writes, pwrites)

    def tr(self, out, in_, ident, reads=(), writes=(), pwrites=()):
        def fn(e):
            return e.transpose(out, in_, ident)
        return self.add("pe", fn, reads, writes, pwrites)

    def act(self, out, in_, func, bias=None, scale=None, accum_out=None,
            reads=(), writes=(), pwrites=()):
        kw = {}
        if bias is not None:
            kw["bias"] = bias
        if scale is not None:
            kw["scale"] = scale
        if accum_out is not None:
            kw["accum_out"] = accum_out

        def fn(e):
            return e.activation(out=out, in_=in_, func=func, **kw)
        return self.add("act", fn, reads, writes, pwrites)

    def tt(self, eng, out, in0, in1, op, reads=(), writes=(), pwrites=()):
        def fn(e):
            return e.tensor_tensor(out=out, in0=in0, in1=in1, op=op)
        return self.add(eng, fn, reads, writes, pwrites)

    def ts(self, eng, out, in0, s1, s2, op0, op1=None, reads=(), writes=(), pwrites=()):
        def fn(e):
            if op1 is None:
                return e.tensor_scalar(out=out, in0=in0, scalar1=s1, scalar2=None, op0=op0)
            return e.tensor_scalar(out=out, in0=in0, scalar1=s1, scalar2=s2, op0=op0, op1=op1)
        return self.add(eng, fn, reads, writes, pwrites)

    def stt(self, out, in0, scalar, in1, op0, op1, reads=(), writes=(), pwrites=()):
        def fn(e):
            return e.scalar_tensor_tensor(out=out, in0=in0, scalar=scalar, in1=in1,
                                          op0=op0, op1=op1)
        return self.add("dve", fn, reads, writes, pwrites)

    def copy(self, eng, out, in_, reads=(), writes=(), pwrites=()):
        if eng == "act":
            return self.act(out, in_, AF.Copy, reads=reads, writes=writes, pwrites=pwrites)

        def fn(e):
            return e.tensor_copy(out=out, in_=in_)
        return self.add(eng, fn, reads, writes, pwrites)

    def recip(self, out, in_, reads=(), writes=(), pwrites=()):
        def fn(e):
            return e.reciprocal(out=out, in_=in_)
        return self.add("dve", fn, reads, writes, pwrites)

    def memset(self, eng, ap, val, writes=(), pwrites=()):
        def fn(e):
            return e.memset(ap, val)
        return self.add(eng, fn, (), writes, pwrites)

    def emit(self, final=False):
        nc = self.nc
        for e in self.ENGS:
            for op in self.ops[e]:
                for d in op.deps:
                    if d.eng == "pe" and op.eng == "pe" and not d.dma:
                        continue
                    d.sig = True
        for e in self.ENGS:
            lst = [o for o in self.ops[e] if not o.dma]
            if lst:
                lst[-1].sig = True
        for op in self.all_dma:
            op.sig = True
        for e in self.ENGS:
            for op in self.ops[e]:
                if not op.sig:
                    continue
                if op.dma:
                    k = self.dnext % NDMASEM
                    self.dnext += 1
                    self.dcnt[k] += 16
                    op.sem = self.dsem[k]
                    op.val = self.dcnt[k]
                    op.idx = ("d", k)
                else:
                    self.ecnt[e] += 1
                    op.sem = self.esem[e]
                    op.val = self.ecnt[e]
                    op.idx = ("e", e)
        prev_on = {}
        for op in sorted(self.all_dma, key=lambda o: o.seq):
            if op.idx in prev_on:
                op.deps.append(prev_on[op.idx])
            prev_on[op.idx] = op
        targets = []
        for e in self.ENGS:
            if self.ecnt[e] > 0:
                targets.append((("e", e), self.esem[e], self.ecnt[e]))
        for k in range(NDMASEM):
            if self.dcnt[k] > 0:
                targets.append((("d", k), self.dsem[k], self.dcnt[k]))
        ops = self.ops
        waited = self.waited
        allops = sorted((o for e in self.ENGS for o in self.ops[e]), key=lambda o: o.seq)
        evc = self.evc
        for op in allops:
            if op.dma:
                vc = {}
            else:
                vc = evc[op.eng]
            for d in op.deps:
                for k, v in d.vc.items():
                    if vc.get(k, 0) < v:
                        vc[k] = v
            if op.sig:
                vc[op.idx] = op.val
            if op.dma:
                op.vc = vc
            else:
                op.vc = dict(vc)

        def run(ename, eng):
            wd = waited[ename]
            for op in ops[ename]:
                need = {}
                for d in op.deps:
                    if d.eng == "pe" and ename == "pe" and not d.dma:
                        continue
                    if wd.get(d.idx, 0) >= d.val:
                        continue
                    if d.idx not in need or need[d.idx].val < d.val:
                        need[d.idx] = d
                rem = list(need.values())
                keep = []
                for d in rem:
                    cov = False
                    for o in rem:
                        if o is not d and o.vc.get(d.idx, 0) >= d.val:
                            cov = True
                            break
                    if not cov:
                        keep.append(d)
                for d in keep:
                    for k, v in d.vc.items():
                        if wd.get(k, 0) < v:
                            wd[k] = v
                wl = [(d.sem, d.val) for d in keep]
                for sem, val in wl[:-1]:
                    eng.wait_ge(sem, val)
                inst = op.fn(eng)
                if wl:
                    inst._wait_ge(wl[-1][0], wl[-1][1])
                if op.sig:
                    inst.then_inc(op.sem, 16 if op.dma else 1)
            for key, sem, val in targets:
                if wd.get(key, 0) < val:
                    eng.wait_ge(sem, val)
                    wd[key] = val

        with nc.Block() as block:
            @block.tensor
            def _(eng):
                run("pe", eng)

            @block.scalar
            def _(eng):
                run("act", eng)

            @block.vector
            def _(eng):
                run("dve", eng)

            @block.gpsimd
            def _(eng):
                run("pool", eng)

            @block.sync
            def _(eng):
                run("sp", eng)
        self.ops = {e: [] for e in self.ENGS}
        self.all_dma = []


def tiles_for(parity, nslot):
    res = []
    for j in range(nslot):
        lo, hi = 2 * j, 2 * j + 1
        a, b = (lo, hi) if j % 2 == 0 else (hi, lo)
        res.append(a if parity == 0 else b)
    return res


def bias_index(nslot):
    idx = {}
    n = 0
    for h in range(4):
        for j in range(nslot):
            for kb in range(8 * j + 8):
                idx[(h, j, kb)] = n
                n += 1
    return idx, n


def build_program(SEQ, phases="A1,A2,B,C", debug=False):
    SQ = SEQ // 2
    NT = SEQ // 512
    NSLOT = SQ // 512
    NKB = SEQ // 128
    bidx, NBIAS = bias_index(NSLOT)
    phases = phases.split(",")

    nc = bass.Bass("TRN2", target_bir_lowering=False)
    dk = "ExternalOutput" if debug else "Internal"

    def din(name, shape, dt):
        return nc.dram_tensor(name, shape, dt, kind="ExternalInput").ap()

    def dscr(name, shape, dt):
        if debug:
            return nc.dram_tensor(name, shape, dt, kind="ExternalOutput").ap()
        return nc.dram_tensor(name, shape, dt).ap()

    xk = din("xk", [SEQ, D], F32)
    xq = din("xq", [SQ, D], F32)
    w_in = din("w_in", [D, 6144], F32)
    w_o_sb = din("w_o_sb", [512, D], F32)
    w_o_df = din("w_o_df", [512, D], F32)
    w_out = din("w_out", [D, D], F32)
    pre_g = din("pre_g", [128, D], F32)
    post_g = din("post_g", [128, D], F32)
    bgate = din("bgate", [128, 16], F32)
    subln = din("subln", [128, 1], F32)
    lam_in = din("lam_in", [128, 4 * HD], F32)
    ident_d = din("ident", [128, 128], BF16)
    negtri_d = din("negtri", [128, 256], BF16)
    ones_d = din("ones", [128, 128], BF16)
    onesf_d = din("onesf", [128, 128], F32)
    strips_d = din("strips", [128, 4 * STRIP_W], BF16)
    bias_d = din("biastab", [128, NBIAS], F32)
    out_d = nc.dram_tensor("out", [SQ, D], F32, kind="ExternalOutput").ap()

    KT = dscr("KT", [1024, SEQ], BF16)
    Vg = dscr("Vg", [8, 128, NKB, 128], BF16)
    QTs = dscr("QTs", [NSLOT, 128, 8, 512], BF16)
    ZTs = dscr("ZTs", [NSLOT, 128, 8, 512], BF16)
    GTs = dscr("GTs", [NSLOT, 128, 16, 512], BF16)
    OTs = dscr("OTs", [NSLOT, 128, 8, 512], BF16)

    uid = [0]
    with ExitStack() as es0:
        S = Sched(nc, es0)

        def phase_A(which):
            with ExitStack() as es:
                def sb(name, shape, dt):
                    uid[0] += 1
                    nm = "sb%d_%s" % (uid[0], name)
                    return T(es.enter_context(nc.sbuf_tensor(nm, shape, dt)), nm)

                def ps(name, shape, dt):
                    uid[0] += 1
                    nm = "ps%d_%s" % (uid[0], name)
                    return T(es.enter_context(nc.psum_tensor(nm, shape, dt)), nm)

                ident = sb("ident", [128, 128], BF16)
                S.dma(ident.t[:], ident_d, writes=[ident])
                gpre = sb("gpre", [128, D], F32)
                S.dma(gpre.t[:], pre_g, writes=[gpre])
                xt = [sb("xt%d" % i, [128, D], F32) for i in range(3)]
                junk = sb("junk", [128, D], BF16)
                ss = [sb("ss%d" % i, [128, 1], F32) for i in range(3)]
                lnv = [sb("lnv%d" % i, [128, 1], F32) for i in range(3)]
                rstd = [sb("rstd%d" % i, [128, 1], F32) for i in range(3)]
                hb = [sb("hb%d" % i, [128, D], BF16) for i in range(2)]
                hT = [sb("hT%d" % i, [128, 8, 512], BF16) for i in range(2)]
                stg = [sb("stg%d" % i, [128, 8, 512], F32) for i in range(2)]
                tp = [ps("tp%d" % i, [128, 1024], BF16) for i in range(2)]
                acc = [ps("pacc%d" % i, [128, 512], F32) for i in range(5)]
                cnt = {"stg": 0, "blk": 0, "acc": 0, "ev": 0}

                def load_w(dst, dcol, src, scol, nch):
                    st = stg[cnt["stg"] % 2]
                    cnt["stg"] += 1
                    S.dma(st.t[:, 0:nch, :],
                          src[:, scol:scol + 512].rearrange("(c p) n -> p c n", p=128),
                          writes=[st])
                    S.copy("dve", dst.t[:, :, dcol:dcol + 512], st.t[:, 0:nch, :],
                           reads=[st], pwrites=[dst])

                if which == 1:
                    xsrc, ntile = xk, NT
                    wk = sb("wk", [128, 8, 1024], BF16)
                    wv = sb("wv", [128, 8, 1024], BF16)
                    load_w(wk, 0, w_in, 512, 8)
                    load_w(wk, 512, w_in, 2560, 8)
                    load_w(wv, 0, w_in, 1024, 8)
                    load_w(wv, 512, w_in, 3072, 8)
                    kst = [sb("kst%d" % i, [128, 8, 512], BF16) for i in range(2)]
                    vst = [sb("vst%d" % i, [128, 8, 4, 128], BF16) for i in range(2)]
                else:
                    xsrc, ntile = xq, NSLOT
                    wq = sb("wq", [128, 8, 2048], BF16)
                    wg = sb("wg", [128, 8, 2048], BF16)
                    load_w(wq, 0, w_in, 0, 8)
                    load_w(wq, 512, w_in, 2048, 8)
                    load_w(wq, 1024, w_in, 1536, 8)
                    load_w(wq, 1536, w_in, 3584, 8)
                    for i in range(4):
                        load_w(wg, 512 * i, w_in, 4096 + 512 * i, 8)
                    bg = sb("bg", [128, 16], F32)
                    S.dma(bg.t[:], bgate, writes=[bg])
                    qst = [sb("qst%d" % i, [128, 8, 512], BF16) for i in range(2)]

                nst = {}

                def normA(tt, sub):
                    i = cnt["blk"]
                    cnt["blk"] += 1
                    nst[(tt, sub)] = i
                    x_ = xt[i % 3]
                    r0 = tt * 512 + sub * 128
                    S.dma(x_.t[:], xsrc[r0:r0 + 128, :], writes=[x_])
                    s_, l_, r_ = ss[i % 3], lnv[i % 3], rstd[i % 3]
                    S.act(junk.t[:], x_.t[:], AF.Square, accum_out=s_.t[:],
                          reads=[x_], writes=[junk, s_])
                    S.act(l_.t[:], s_.t[:], AF.Ln, bias=EPS, scale=1.0 / D,
                          reads=[s_], writes=[l_])
                    S.act(r_.t[:], l_.t[:], AF.Exp, scale=-0.5, reads=[l_], writes=[r_])
                    hb_ = hb[i % 2]
                    S.stt(hb_.t[:], x_.t[:], r_.t[:], gpre.t[:], ALU.mult, ALU.mult,
                          reads=[x_, r_, gpre], writes=[hb_])

                def normB(tt, sub):
                    i = nst[(tt, sub)]
                    h = hT[tt % 2]
                    hb_ = hb[i % 2]
                    tp_ = tp[i % 2]
                    for c in range(8):
                        S.tr(tp_.t[:, c * 128:(c + 1) * 128], hb_.t[:, c * 128:(c + 1) * 128],
                             ident.t[:], reads=[hb_, ident], pwrites=[tp_])
                    S.copy("dve" if sub % 2 == 0 else "act",
                           h.t[:, :, sub * 128:(sub + 1) * 128],
                           tp_.t[:].rearrange("p (c n) -> p c n", c=8),
                           reads=[tp_], pwrites=[h])

                def evac(eng, out, in_, func, bias, scale, reads, pwrites):
                    if func is None and eng == "dve":
                        if scale is None:
                            S.copy("dve", out, in_, reads=reads, pwrites=pwrites)
                        else:
                            S.ts("dve", out, in_, scale, None, ALU.mult, reads=reads,
                                 pwrites=pwrites)
                    else:
                        S.act(out, in_, func if func is not None else AF.Copy, bias=bias,
                              scale=scale, reads=reads, pwrites=pwrites)

                def proj1(tt):
                    h = hT[tt % 2]
                    k_ = kst[tt % 2]
                    v_ = vst[tt % 2]
                    for kc in range(8):
                        a_ = acc[cnt["acc"] % 5]
                        cnt["acc"] += 1
                        for c in range(8):
                            S.mm(a_.t[:], wk.t[:, c, kc * 128:(kc + 1) * 128], h.t[:, c, :],
                                 c == 0, c == 7, reads=[wk, h], pwrites=[a_])
                        eng = "dve" if cnt["ev"] % 2 == 0 else "act"
                        cnt["ev"] += 1
                        evac(eng, k_.t[:, kc, :], a_.t[:], None, None, None, [a_], [k_])
                        yield
                    S.dma(KT[:, tt * 512:(tt + 1) * 512].rearrange("(c p) n -> p c n", p=128),
                          k_.t[:], reads=[k_])
                    for sub in range(4):
                        for slab in range(2):
                            a_ = acc[cnt["acc"] % 5]
                            cnt["acc"] += 1
                            for c in range(8):
                                S.mm(a_.t[:], h.t[:, c, sub * 128:(sub + 1) * 128],
                                     wv.t[:, c, slab * 512:(slab + 1) * 512],
                                     c == 0, c == 7, reads=[wv, h], pwrites=[a_])
                            eng = "dve" if cnt["ev"] % 2 == 0 else "act"
                            cnt["ev"] += 1
                            evac(eng, v_.t[:, slab * 4:(slab + 1) * 4, sub, :],
                                 a_.t[:].rearrange("p (g n) -> p g n", g=4),
                                 None, None, None, [a_], [v_])
                            yield
                    S.dma(Vg[:, :, tt * 4:(tt + 1) * 4, :].rearrange("g p s n -> p g s n"),
                          v_.t[:], reads=[v_])

                def proj2(tt):
                    h = hT[tt % 2]
                    for grp in range(4):
                        q_ = qst[(tt * 4 + grp) % 2]
                        for oc8 in range(8):
                            oc = grp * 8 + oc8
                            a_ = acc[cnt["acc"] % 5]
                            cnt["acc"] += 1
                            wsrc, wc = (wq, oc) if oc < 16 else (wg, oc - 16)
                            for c in range(8):
                                S.mm(a_.t[:], wsrc.t[:, c, wc * 128:(wc + 1) * 128], h.t[:, c, :],
                                     c == 0, c == 7, reads=[wsrc, h], pwrites=[a_])
                            if grp == 0:
                                eng = "dve" if oc8 % 2 == 0 else "act"
                                evac(eng, q_.t[:, oc8, :], a_.t[:], None, None, 0.125, [a_], [q_])
                            elif grp == 1:
                                evac("act", q_.t[:, oc8, :], a_.t[:], AF.Silu, None, None,
                                     [a_], [q_])
                            else:
                                gc = oc - 16
                                evac("act", q_.t[:, oc8, :], a_.t[:], AF.Sigmoid,
                                     bg.t[:, gc:gc + 1], None, [a_, bg], [q_])
                            yield
                        dst = (QTs[tt], ZTs[tt], GTs[tt, :, 0:8, :], GTs[tt, :, 8:16, :])[grp]
                        S.dma(dst, q_.t[:], reads=[q_])

                proj = proj1 if which == 1 else proj2
                G = 16 if which == 1 else 32
                marks = {1 + (G * s_) // 4: s_ for s_ in range(4)}
                for sub in range(4):
                    normA(0, sub)
                    normB(0, sub)
                for tt in range(ntile):
                    gi = 0
                    nxt = tt + 1 < ntile
                    for _ in proj(tt):
                        gi += 1
                        if nxt and gi in marks:
                            sub = marks[gi]
                            normA(tt + 1, sub)
                            if sub > 0:
                                normB(tt + 1, sub - 1)
                    if nxt:
                        normB(tt + 1, 3)
                S.emit()

        if "A1" in phases:
            phase_A(1)
        if "A2" in phases:
            phase_A(2)

        def phase_B2():
            from collections import deque
            with ExitStack() as es:
                def sb(name, shape, dt):
                    uid[0] += 1
                    nm = "sb%d_%s" % (uid[0], name)
                    return T(es.enter_context(nc.sbuf_tensor(nm, shape, dt)), nm)

                def ps(name, shape, dt):
                    uid[0] += 1
                    nm = "ps%d_%s" % (uid[0], name)
                    return T(es.enter_context(nc.psum_tensor(nm, shape, dt)), nm)

                negtri = sb("negtri", [128, 256], BF16)
                S.dma(negtri.t[:], negtri_d, writes=[negtri])
                ones = sb("ones", [128, 128], BF16)
                S.dma(ones.t[:], ones_d, writes=[ones])
                onesf = sb("onesf", [128, 128], F32)
                S.dma(onesf.t[:], onesf_d, writes=[onesf])
                strips = sb("strips", [128, 4 * STRIP_W], BF16)
                S.dma(strips.t[:], strips_d, writes=[strips])
                biast = sb("biast", [128, NBIAS], F32)
                S.dma(biast.t[:], bias_d, writes=[biast])
                sublc = sb("sublc", [128, 1], F32)
                S.dma(sublc.t[:], subln, writes=[sublc])
                lamv = sb("lamv", [128, 4 * HD], F32)
                S.dma(lamv.t[:], lam_in, writes=[lamv])
                lprod = sb("lprod", [128, 2 * HD], F32)
                lsum = sb("lsum", [128, 2], F32)
                lexp = sb("lexp", [128, 2], F32)
                neglam = sb("neglam", [128, 1], F32)
                gcol = sb("gcol", [128, 1], F32)
                S.tt("dve", lprod.t[:, 0:HD], lamv.t[:, 0:HD], lamv.t[:, HD:2 * HD], ALU.mult,
                     reads=[lamv], pwrites=[lprod])
                S.tt("dve", lprod.t[:, HD:2 * HD], lamv.t[:, 2 * HD:3 * HD], lamv.t[:, 3 * HD:4 * HD],
                     ALU.mult, reads=[lamv], pwrites=[lprod])

                def red(e):
                    return e.tensor_reduce(out=lsum.t[:], in_=lprod.t[:].rearrange("p (a b) -> p a b", a=2),
                                           axis=mybir.AxisListType.X, op=ALU.add)
                S.add("dve", red, reads=[lprod], writes=[lsum])
                S.act(lexp.t[:], lsum.t[:], AF.Exp, reads=[lsum], writes=[lexp])
                S.stt(neglam.t[:], lexp.t[:, 1:2], -LAMBDA_INIT, lexp.t[:, 0:1], ALU.add, ALU.subtract,
                      reads=[lexp], writes=[neglam])
                S.ts("dve", gcol.t[:], sublc.t[:], 1.0 - LAMBDA_INIT, None, ALU.mult,
                     reads=[sublc], writes=[gcol])

                NCH = 4
                CH = NKB // NCH
                kTs = sb("kTs", [128, SEQ], BF16)
                vvs = sb("vvs", [128, NKB, 128], BF16)
                kTd = sb("kTd", [128, SEQ], BF16)
                vvd = sb("vvd", [128, NKB, 128], BF16)
                kTs_b = [Buf() for _ in range(NCH)]
                vvs_b = [Buf() for _ in range(NCH)]
                kTd_b = [Buf() for _ in range(NCH)]
                vvd_b = [Buf() for _ in range(NCH)]
                qs_ = [sb("qs%d" % i, [128, 512], BF16) for i in range(3)]
                zs_ = [sb("zs%d" % i, [128, 512], BF16) for i in range(3)]
                qd_ = [sb("qd%d" % i, [128, 512], BF16) for i in range(3)]
                zd_ = [sb("zd%d" % i, [128, 512], BF16) for i in range(3)]
                os_ = [sb("os%d" % i, [128, 512], BF16) for i in range(2)]
                od_ = [sb("od%d" % i, [128, 512], BF16) for i in range(2)]
                NW = 4
                e_t = [sb("e%d" % i, [128, 512], F32) for i in range(NW)]
                sp_t = [sb("sp%d" % i, [128, 512], BF16) for i in range(NW)]
                acc_t = [sb("ac%d" % i, [128, 512], BF16) for i in range(NW)]
                a_t = [sb("a%d" % i, [128, 512], BF16) for i in range(NW)]
                NP = 3
                p_t = [sb("p%d" % i, [128, 512], BF16) for i in range(NP)]
                NZS = 3
                zsb = [ps("zsb%d" % i, [128, 512], F32) for i in range(NZS)]
                NZD = 1
                zdf = [ps("zdf%d" % i, [128, 512], F32) for i in range(NZD)]
                subps = ps("subps", [128, 512], F32)
                obs = ps("obs", [128, 512], F32)
                obs_b = [Buf(), Buf()]
                obd = ps("obd", [128, 512], F32)
                lbd = ps("lbd", [128, 512], F32)
                uraw = [[sb("uraw%d_%d" % (i, m), [128, 512], F32) for m in range(2)] for i in range(2)]
                lraw = [[sb("lraw%d_%d" % (i, m), [128, 512], F32) for m in range(2)] for i in range(2)]
                r_t = [sb("r%d" % i, [128, 512], F32) for i in range(2)]
                u_t = [sb("u%d" % i, [128, 512], F32) for i in range(3)]
                cnt = {"zs": 0, "zd": 0, "w": 0, "p": 0}
                bg = deque()
                for t_ in sp_t + a_t + p_t:
                    S.memset("pool", t_.t[:], 0.0, writes=[t_])

                def ucols(kb, nkb):
                    i = kb - (nkb - 4)
                    return slice(128 * i, 512) if i > 0 else slice(0, 512)

                def strip(which_set, which_sb, sub, nonstrict):
                    k = which_set * 2 + which_sb
                    off = k * STRIP_W + 384 - 128 * sub + (1 if nonstrict else 0)
                    return strips.t[:, off:off + 512]

                def load_pair(g):
                    gd = 4 + g
                    for c in range(NCH - 1, -1, -1):
                        ks = slice(c * CH * 128, (c + 1) * CH * 128)
                        S.dma(kTs.t[:, ks], KT[g * 128:(g + 1) * 128, ks], writes=[kTs_b[c]], q="sp")
                        S.dma(vvs.t[:, c * CH:(c + 1) * CH, :], Vg[g, :, c * CH:(c + 1) * CH, :],
                              writes=[vvs_b[c]], q="sp")
                        S.dma(kTd.t[:, ks], KT[gd * 128:(gd + 1) * 128, ks], writes=[kTd_b[c]], q="sp")
                        S.dma(vvd.t[:, c * CH:(c + 1) * CH, :], Vg[gd, :, c * CH:(c + 1) * CH, :],
                              writes=[vvd_b[c]], q="sp")

                def load_slot(g, j):
                    gd = 4 + g
                    i = j % 3
                    cs = slice(j * 512, (j + 1) * 512)
                    S.dma(qs_[i].t[:], QTs[j, :, g, :], writes=[qs_[i]])
                    S.dma(zs_[i].t[:], ZTs[j, :, g, :], writes=[zs_[i]])
                    S.dma(qd_[i].t[:], QTs[j, :, gd, :], writes=[qd_[i]])
                    S.dma(zd_[i].t[:], ZTs[j, :, gd, :], writes=[zd_[i]])

                def run_pair(g):
                    gd = 4 + g
                    units = []
                    order = list(range(NSLOT - 1, -1, -1))
                    for j in order:
                        nkb = 8 * j + 8
                        for half in range(2):
                            for n in range(nkb):
                                units.append((j, half, n, nkb))
                    NU = len(units)
                    st = [None] * NU
                    load_slot(g, order[0])
                    d_cut = 120.0 / SLOPES[g]

                    def kbmin(j):
                        return max(0, int((1024 * j - d_cut) // 128))

                    def s0(u):
                        j, half, n, nkb = units[u]
                        if n == 0 and half == 0:
                            oi = order.index(j)
                            if oi + 1 < len(order):
                                load_slot(g, order[oi + 1])
                        kb = nkb - 1 - n
                        sl = j % 2
                        prt = slice(half * 64, (half + 1) * 64)
                        z = zsb[cnt["zs"] % NZS]
                        cnt["zs"] += 1
                        w = cnt["w"] % NW
                        cnt["w"] += 1
                        st[u] = {"z": z, "w": w}
                        S.mm(z.t[:], kTs.t[prt, kb * 128:(kb + 1) * 128], qs_[j % 3].t[prt, :],
                             True, True, reads=[kTs_b[kb // CH], qs_[j % 3]], pwrites=[z])

                    def s1(u):
                        j, half, n, nkb = units[u]
                        cs = ucols(nkb - 1 - n, nkb)
                        z, w = st[u]["z"], st[u]["w"]
                        S.act(e_t[w].t[:, cs], z.t[:, cs], AF.Exp, reads=[z], writes=[e_t[w]])

                    def s2(u):
                        j, half, n, nkb = units[u]
                        kb = nkb - 1 - n
                        z, w = st[u]["z"], st[u]["w"]
                        cs = ucols(kb, nkb)
                        S.act(sp_t[w].t[:, cs], e_t[w].t[:, cs], AF.Ln, bias=1.0, scale=1.0,
                              reads=[e_t[w]], writes=[sp_t[w]])
                        if kb >= nkb - 8:
                            rel = kb - (nkb - 8)
                            m = strip(j % 2, rel // 4, rel % 4, False)
                            S.tt("dve", sp_t[w].t[:], sp_t[w].t[:], m, ALU.mult,
                                 reads=[strips, sp_t[w]], writes=[sp_t[w]])

                    def s3(u):
                        j, half, n, nkb = units[u]
                        z, w = st[u]["z"], st[u]["w"]
                        S.mm(z.t[:], negtri.t[:, 0:128], sp_t[w].t[:], False, n == 0,
                             reads=[negtri, sp_t[w]], pwrites=[z], skip=True)
                        if n > 0:
                            pw = st[u - 1]["w"]
                            S.mm(z.t[:], negtri.t[:, 128:256], acc_t[pw].t[:], False, True,
                                 reads=[negtri, acc_t[pw]], pwrites=[z], skip=True)
                        if n + 1 < nkb:
                            if n == 0:
                                S.copy("dve", acc_t[w].t[:], sp_t[w].t[:], reads=[sp_t[w]],
                                       writes=[acc_t[w]])
                            else:
                                pw = st[u - 1]["w"]
                                S.tt("dve", acc_t[w].t[:], acc_t[pw].t[:], sp_t[w].t[:], ALU.add,
                                     reads=[acc_t[pw], sp_t[w]], writes=[acc_t[w]])

                    def s4(u):
                        j, half, n, nkb = units[u]
                        kb = nkb - 1 - n
                        z, w = st[u]["z"], st[u]["w"]
                        cs = ucols(kb, nkb)
                        S.act(a_t[w].t[:, cs], z.t[:, cs], AF.Exp, reads=[z], writes=[a_t[w]])
                        if kb >= nkb - 8:
                            rel = kb - (nkb - 8)
                            m = strip(j % 2, rel // 4, rel % 4, False)
                            S.tt("dve", a_t[w].t[:], a_t[w].t[:], m, ALU.mult,
                                 reads=[strips, a_t[w]], writes=[a_t[w]])

                    def s5(u):
                        j, half, n, nkb = units[u]
                        kb = nkb - 1 - n
                        sl = j % 2
                        prt = slice(half * 64, (half + 1) * 64)
                        w = st[u]["w"]
                        S.mm(obs.t[prt, :], vvs.t[:, kb, half * 64:(half + 1) * 64], a_t[w].t[:],
                             n == 0, n == nkb - 1, reads=[vvs_b[kb // CH], a_t[w]], pwrites=[obs_b[half]])
                        if n == nkb - 1:
                            S.tt("dve", os_[sl].t[prt, :], obs.t[prt, :], zs_[j % 3].t[prt, :], ALU.mult,
                                 reads=[obs_b[half], zs_[j % 3]], pwrites=[os_[sl]])
                            if half == 1:
                                S.dma(OTs[j, :, g, :], os_[sl].t[:], reads=[os_[sl]])

                    def d0(u):
                        j, half, n, nkb = units[u]
                        if n < kbmin(j):
                            return
                        kb = n
                        sl = j % 2
                        prt = slice((1 - half) * 64, (2 - half) * 64)
                        zd = zdf[cnt["zd"] % NZD]
                        cnt["zd"] += 1
                        wp = cnt["p"] % NP
                        cnt["p"] += 1
                        st[u]["zd"] = zd
                        st[u]["wp"] = wp
                        S.mm(zd.t[:], kTd.t[prt, kb * 128:(kb + 1) * 128], qd_[j % 3].t[prt, :],
                             True, True, reads=[kTd_b[kb // CH], qd_[j % 3]], pwrites=[zd])

                    def d1(u):
                        j, half, n, nkb = units[u]
                        if n < kbmin(j):
                            return
                        kb = n
                        zd, wp = st[u]["zd"], st[u]["wp"]
                        bc = bidx[(g, j, kb)]
                        cs = ucols(kb, nkb)
                        S.act(p_t[wp].t[:, cs], zd.t[:, cs], AF.Exp, reads=[zd], writes=[p_t[wp]])
                        if kb >= nkb - 8:
                            rel = kb - (nkb - 8)
                            mk = strip(j % 2, rel // 4, rel % 4, True)
                            S.stt(p_t[wp].t[:], p_t[wp].t[:], biast.t[:, bc:bc + 1], mk, ALU.mult, ALU.mult,
                                  reads=[strips, biast, p_t[wp]], writes=[p_t[wp]])
                        else:
                            S.ts("dve", p_t[wp].t[:], p_t[wp].t[:], biast.t[:, bc:bc + 1], None, ALU.mult,
                                 reads=[biast, p_t[wp]], writes=[p_t[wp]])

                    def slot_epilogue(j):
                        sl = j % 2
                        ur, lr = uraw[sl], lraw[sl]
                        ops = []
                        for m in range(2):
                            for q4 in range(4):
                                cs = slice(q4 * 128, (q4 + 1) * 128)
                                ops.append(lambda m=m, cs=cs: S.recip(r_t[m].t[:, cs], lr[m].t[:, cs],
                                                                     reads=[lr[m]], pwrites=[r_t[m]]))
                            ops.append(lambda m=m: S.tt("dve", u_t[m].t[:], ur[m].t[:], r_t[m].t[:], ALU.mult,
                                                        reads=[ur[m], r_t[m]], writes=[u_t[m]]))
                        ops.append(lambda: S.stt(u_t[2].t[:], u_t[1].t[:], neglam.t[:], u_t[0].t[:],
                                                 ALU.mult, ALU.add,
                                                 reads=[u_t[0], u_t[1], neglam], writes=[u_t[2]]))

                        def tail():
                            S.act(r_t[0].t[:], u_t[2].t[:], AF.Square, reads=[u_t[2]], writes=[r_t[0]])
                            zq = subps
                            S.mm(zq.t[:], onesf.t[:], r_t[0].t[:], True, True, reads=[onesf, r_t[0]],
                                 pwrites=[zq])
                            S.act(r_t[1].t[:], zq.t[:], AF.Ln, bias=EPS, scale=1.0, reads=[zq],
                                  writes=[r_t[1]])
                            S.act(r_t[0].t[:], r_t[1].t[:], AF.Exp, scale=-0.5, reads=[r_t[1]],
                                  writes=[r_t[0]])
                            S.tt("dve", u_t[0].t[:], u_t[2].t[:], r_t[0].t[:], ALU.mult,
                                 reads=[u_t[2], r_t[0]], writes=[u_t[0]])
                            S.stt(od_[sl].t[:], u_t[0].t[:], gcol.t[:], zd_[j % 3].t[:], ALU.mult, ALU.mult,
                                  reads=[u_t[0], gcol, zd_[j % 3]], writes=[od_[sl]])
                            S.dma(OTs[j, :, gd, :], od_[sl].t[:], reads=[od_[sl]])
                        ops.append(tail)
                        bg.extend(ops)

                    def d2(u):
                        j, half, n, nkb = units[u]
                        if n < kbmin(j):
                            return
                        kb = n
                        sl = j % 2
                        wp = st[u]["wp"]
                        first = (n == kbmin(j))
                        S.mm(obd.t[:], vvd.t[:, kb, :], p_t[wp].t[:], first, n == nkb - 1,
                             reads=[vvd_b[kb // CH], p_t[wp]], pwrites=[obd])
                        S.mm(lbd.t[:], ones.t[:], p_t[wp].t[:], first, n == nkb - 1,
                             reads=[ones, p_t[wp]], pwrites=[lbd])
                        if n == nkb - 1:
                            S.copy("dve", uraw[sl][1 - half].t[:], obd.t[:], reads=[obd],
                                   writes=[uraw[sl][1 - half]])
                            S.copy("dve", lraw[sl][1 - half].t[:], lbd.t[:], reads=[lbd],
                                   writes=[lraw[sl][1 - half]])
                            if half == 1:
                                slot_epilogue(j)

                    sched = [(s5, 4), (s4, 3), (s3, 2), (d2, 2), (s1, 1), (d1, 1), (s2, 1), (s0, 0), (d0, 0)]
                    for step in range(NU + 5):
                        for fn, sk in sched:
                            u = step - sk
                            if 0 <= u < NU:
                                fn(u)
                        for _ in range(1):
                            if bg:
                                bg.popleft()()
                    while bg:
                        bg.popleft()()

                load_pair(0)
                for g in range(4):
                    run_pair(g)
                    if g + 1 < 4:
                        load_pair(g + 1)
                S.emit()

        if "B" in phases:
            phase_B2()

        def phase_C():
            with ExitStack() as es:
                def sb(name, shape, dt):
                    uid[0] += 1
                    nm = "sb%d_%s" % (uid[0], name)
                    return T(es.enter_context(nc.sbuf_tensor(nm, shape, dt)), nm)

                def ps(name, shape, dt):
                    uid[0] += 1
                    nm = "ps%d_%s" % (uid[0], name)
                    return T(es.enter_context(nc.psum_tensor(nm, shape, dt)), nm)

                stg = [sb("stgc%d" % i, [128, 8, 512], F32) for i in range(2)]
                wosb = sb("wosb", [128, 4, 1024], BF16)
                wodf = sb("wodf", [128, 4, 1024], BF16)
                wout = sb("wout", [128, 8, 1024], BF16)
                pg = sb("pg", [128, D], F32)
                S.dma(pg.t[:], post_g, writes=[pg])
                n = 0
                for dst, src, nch in ((wosb, w_o_sb, 4), (wodf, w_o_df, 4), (wout, w_out, 8)):
                    for half in range(2):
                        st = stg[n % 2]
                        n += 1
                        S.dma(st.t[:, 0:nch, :],
                              src[:, half * 512:(half + 1) * 512].rearrange("(c p) n -> p c n", p=128),
                              writes=[st])
                        S.copy("dve", dst.t[:, :, half * 512:(half + 1) * 512], st.t[:, 0:nch, :],
                               reads=[st], pwrites=[dst])
                oTt = [sb("oTt%d" % i, [128, 8, 512], BF16) for i in range(2)]
                Gt = [sb("Gt%d" % i, [128, 16, 512], BF16) for i in range(2)]
                xr = [sb("xr%d" % i, [128, 4, D], F32) for i in range(2)]
                m1 = [sb("m1_%d" % i, [128, 512], F32) for i in range(2)]
                m2 = [sb("m2_%d" % i, [128, 512], F32) for i in range(2)]
                mg = [sb("mg%d" % i, [128, 8, 512], BF16) for i in range(2)]
                ot = [sb("ot%d" % i, [128, D], F32) for i in range(2)]
                t1 = [sb("t1_%d" % i, [128, D], F32) for i in range(2)]
                junk = sb("junkc", [128, 512], BF16)
                ssa = [sb("ssa%d" % i, [128, 2], F32) for i in range(2)]
                sst = [sb("sst%d" % i, [128, 1], F32) for i in range(2)]
                lnv = [sb("lnvc%d" % i, [128, 1], F32) for i in range(2)]
                rstd = [sb("rstdc%d" % i, [128, 1], F32) for i in range(2)]
                py = [ps("py%d" % i, [128, 512], F32) for i in range(4)]
                po = [ps("po%d" % i, [128, 512], F32) for i in range(4)]
                cnt = {"y": 0, "o": 0, "m": 0, "s": 0}

                def load(j):
                    i = j % 2
                    cs = slice(j * 512, (j + 1) * 512)
                    S.dma(oTt[i].t[:], OTs[j], writes=[oTt[i]])
                    S.dma(Gt[i].t[:], GTs[j], writes=[Gt[i]])
                    S.dma(xr[i].t[:], xq[cs, :].rearrange("(s p) n -> p s n", p=128), writes=[xr[i]])

                def tile(j):
                    i = j % 2
                    o_, g_, x_, mg_ = oTt[i], Gt[i], xr[i], mg[i]
                    for dc in range(8):
                        ya = py[cnt["y"] % 4]
                        yb = py[(cnt["y"] + 1) % 4]
                        cnt["y"] += 2
                        for ec in range(4):
                            S.mm(ya.t[:], wosb.t[:, ec, dc * 128:(dc + 1) * 128], o_.t[:, ec, :],
                                 ec == 0, ec == 3, reads=[wosb, o_], pwrites=[ya])
                        for ec in range(4):
                            S.mm(yb.t[:], wodf.t[:, ec, dc * 128:(dc + 1) * 128], o_.t[:, 4 + ec, :],
                                 ec == 0, ec == 3, reads=[wodf, o_], pwrites=[yb])
                        k = cnt["m"] % 2
                        cnt["m"] += 1
                        S.tt("dve", m1[k].t[:], ya.t[:], g_.t[:, dc, :], ALU.mult,
                             reads=[ya, g_], writes=[m1[k]])
                        S.tt("dve", m2[k].t[:], yb.t[:], g_.t[:, 8 + dc, :], ALU.mult,
                             reads=[yb, g_], writes=[m2[k]])
                        S.tt("pool", mg_.t[:, dc, :], m1[k].t[:], m2[k].t[:], ALU.add,
                             reads=[m1[k], m2[k]], pwrites=[mg_])
                    for sub in range(4):
                        k = cnt["s"] % 2
                        cnt["s"] += 1
                        pa = po[cnt["o"] % 4]
                        pb = po[(cnt["o"] + 1) % 4]
                        cnt["o"] += 2
                        for half, p_ in ((0, pa), (1, pb)):
                            for dc in range(8):
                                S.mm(p_.t[:], mg_.t[:, dc, sub * 128:(sub + 1) * 128],
                                     wout.t[:, dc, half * 512:(half + 1) * 512],
                                     dc == 0, dc == 7, reads=[mg_, wout], pwrites=[p_])
                        S.act(junk.t[:], pa.t[:], AF.Square, accum_out=ssa[k].t[:, 0:1],
                              reads=[pa], writes=[junk], pwrites=[ssa[k]])
                        S.act(junk.t[:], pb.t[:], AF.Square, accum_out=ssa[k].t[:, 1:2],
                              reads=[pb], writes=[junk], pwrites=[ssa[k]])
                        S.tt("dve", sst[k].t[:], ssa[k].t[:, 0:1], ssa[k].t[:, 1:2], ALU.add,
                             reads=[ssa[k]], writes=[sst[k]])
                        S.act(lnv[k].t[:], sst[k].t[:], AF.Ln, bias=EPS, scale=1.0 / D,
                              reads=[sst[k]], writes=[lnv[k]])
                        S.act(rstd[k].t[:], lnv[k].t[:], AF.Exp, scale=-0.5, reads=[lnv[k]],
                              writes=[rstd[k]])
                        for half, p_ in ((0, pa), (1, pb)):
                            hs = slice(half * 512, (half + 1) * 512)
                            S.stt(t1[k].t[:, hs], p_.t[:], rstd[k].t[:], pg.t[:, hs], ALU.mult, ALU.mult,
                                  reads=[p_, rstd[k], pg], pwrites=[t1[k]])
                        S.tt("pool", ot[k].t[:], t1[k].t[:], x_.t[:, sub, :], ALU.add,
                             reads=[t1[k], x_], writes=[ot[k]])
                        r0 = j * 512 + sub * 128
                        S.dma(out_d[r0:r0 + 128, :], ot[k].t[:], reads=[ot[k]])

                load(0)
                for j in range(NSLOT):
                    if j + 1 < NSLOT:
                        load(j + 1)
                    tile(j)
                S.emit()

        if "C" in phases:
            phase_C()
    return nc


def make_consts(SEQ, parity):
    SQ = SEQ // 2
    NSLOT = SQ // 512
    bf = ml_dtypes.bfloat16
    ident = np.eye(128, dtype=np.float32).astype(bf)
    jj = np.arange(128)[:, None]
    s_ = np.arange(128)[None, :]
    negtri = np.concatenate([-(jj >= s_).astype(np.float32), -np.ones((128, 128), np.float32)], axis=1).astype(bf)
    ones = np.ones((128, 128), np.float32).astype(bf)
    onesf = np.full((128, 128), 1.0 / 128, np.float32)
    xx = np.arange(STRIP_W)[None, :]
    ss = np.arange(128)[:, None]
    tri = ((xx - 384) > ss).astype(np.float32)
    zer = np.zeros((128, STRIP_W), np.float32)
    one = np.ones((128, STRIP_W), np.float32)
    lo_set = [tri, zer]
    hi_set = [one, tri]
    sets = [lo_set, hi_set] if parity == 0 else [hi_set, lo_set]
    strips = np.concatenate([sets[0][0], sets[0][1], sets[1][0], sets[1][1]], axis=1).astype(bf)
    bidx, nb = bias_index(NSLOT)
    tiles = tiles_for(parity, NSLOT)
    bias = np.zeros((128, nb), np.float32)
    sl = np.arange(128, dtype=np.float64)
    for (h, j, kb), c in bidx.items():
        tref = 512 * tiles[j] + 256
        bias[:, c] = np.exp(SLOPES[h] * (128 * kb + sl - tref)).astype(np.float32)
        if kb // 4 > tiles[j]:
            bias[:, c] = 0.0
    return dict(ident=ident, negtri=negtri, ones=ones, onesf=onesf, strips=strips, biastab=bias)


def make_in_maps(SEQ, x, pre_norm_g, w_in, b_gate, lambda_q1, lambda_k1, lambda_q2, lambda_k2,
                 subln_g, w_o_sb, w_o_diff, w_out, post_norm_g):
    f = lambda a: np.ascontiguousarray(np.asarray(a, dtype=np.float32))
    x = f(x)
    SQ = SEQ // 2
    NSLOT = SQ // 512
    shared = dict(
        w_in=f(w_in[0]), w_o_sb=f(w_o_sb[0]), w_o_df=f(w_o_diff[0]), w_out=f(w_out[0]),
        pre_g=f(np.broadcast_to(np.asarray(pre_norm_g[0])[None, :], (128, D))),
        post_g=f(np.broadcast_to(np.asarray(post_norm_g[0])[None, :], (128, D))),
        bgate=f(np.asarray(b_gate[0]).reshape(16, 128).T),
        subln=f(np.asarray(subln_g[0]).reshape(128, 1)),
        lam_in=f(np.broadcast_to(np.concatenate([np.asarray(lambda_q1[0]), np.asarray(lambda_k1[0]),
                                                 np.asarray(lambda_q2[0]), np.asarray(lambda_k2[0])])[None, :],
                                 (128, 4 * HD))),
    )
    consts = [make_consts(SEQ, 0), make_consts(SEQ, 1)]
    in_maps = []
    for c in range(NCORES):
        b, par = c // 2, c % 2
        tiles = tiles_for(par, NSLOT)
        xb = x[b]
        xq = np.concatenate([xb[t * 512:(t + 1) * 512] for t in tiles], axis=0)
        m = dict(shared)
        m.update(consts[par])
        m["xk"] = np.ascontiguousarray(xb)
        m["xq"] = np.ascontiguousarray(xq)
        in_maps.append(m)
    return in_maps


_NC_CACHE = {}


def kernel(x, pre_norm_g, w_in, b_gate, lambda_q1, lambda_k1, lambda_q2, lambda_k2,
           subln_g, w_o_sb, w_o_diff, w_out, post_norm_g):
    x = np.asarray(x)
    B, SEQ, _ = x.shape
    SQ = SEQ // 2
    NSLOT = SQ // 512
    in_maps = make_in_maps(SEQ, x, pre_norm_g, w_in, b_gate, lambda_q1, lambda_k1, lambda_q2,
                           lambda_k2, subln_g, w_o_sb, w_o_diff, w_out, post_norm_g)
    if SEQ not in _NC_CACHE:
        _NC_CACHE[SEQ] = build_program(SEQ)
    nc = _NC_CACHE[SEQ]
    res = run_bass_kernel_spmd(nc, in_maps, core_ids=list(range(NCORES)))
    out = np.empty((B, SEQ, D), np.float32)
    for c in range(NCORES):
        b, par = c // 2, c % 2
        tiles = tiles_for(par, NSLOT)
        o = np.asarray(res.results[c]["out"]).reshape(SQ, D)
        for j, t in enumerate(tiles):
            out[b, t * 512:(t + 1) * 512] = o[j * 512:(j + 1) * 512]
    return out
```

```python
import math
from contextlib import ExitStack

import numpy as np
import ml_dtypes

import concourse.bass as bass
import concourse.mybir as mybir
from concourse.bass_utils import run_bass_kernel_spmd

F32 = mybir.dt.float32
BF16 = mybir.dt.bfloat16
AF = mybir.ActivationFunctionType
ALU = mybir.AluOpType

D = 1024
HD = 64
NCORES = 8
EPS = 1e-6
LAMBDA_INIT = 0.8 - 0.6 * math.exp(-0.3 * 0)
SLOPES = [2.0 ** (-8.0 * (h + 1) / 4) for h in range(4)]
STRIP_W = 897
NDMASEM = 24


class Buf:
    __slots__ = ("name", "w", "r", "war")

    def __init__(self, name=""):
        self.name = name
        self.w = []
        self.r = []
        self.war = []


class T:
    __slots__ = ("t", "b")

    def __init__(self, t, name=""):
        self.t = t
        self.b = Buf(name)


class Op:
    __slots__ = ("eng", "fn", "deps", "dma", "sig", "sem", "val", "idx", "seq", "vc")

    def __init__(self, eng, fn, dma):
        self.eng = eng
        self.fn = fn
        self.dma = dma
        self.deps = []
        self.sig = False
        self.sem = None
        self.val = 0
        self.idx = 0


def _b(x):
    return x.b if isinstance(x, T) else x


class Sched:
    ENGS = ("pe", "act", "dve", "pool", "sp")

    def __init__(self, nc, es):
        self.nc = nc
        self.ops = {e: [] for e in self.ENGS}
        self.esem = {e: es.enter_context(nc.semaphore("s_" + e)) for e in self.ENGS}
        self.ecnt = {e: 0 for e in self.ENGS}
        self.dsem = [es.enter_context(nc.semaphore("s_dma%d" % i)) for i in range(NDMASEM)]
        self.dcnt = [0] * NDMASEM
        self.dnext = 0
        self.waited = {e: {} for e in self.ENGS}
        self.all_dma = []
        self.seq = 0
        self.evc = {e: {} for e in self.ENGS}

    def add(self, eng, fn, reads=(), writes=(), pwrites=(), dma=False):
        op = Op(eng, fn, dma)
        self.seq += 1
        op.seq = self.seq
        deps = []
        for x in reads:
            b = _b(x)
            deps.extend(b.w)
        for x in writes:
            b = _b(x)
            deps.extend(b.w)
            deps.extend(b.r)
            deps.extend(b.war)
        for x in pwrites:
            b = _b(x)
            if b.r:
                b.war = list(b.r)
                b.r = []
                b.w = []
            deps.extend(b.war)
        op.deps = deps
        for x in reads:
            b = _b(x)
            if not dma:
                b.r = [o for o in b.r if o.dma or o.eng != eng]
            b.r.append(op)
        for x in writes:
            b = _b(x)
            b.w = [op]
            b.r = []
            b.war = []
        for x in pwrites:
            b = _b(x)
            if not dma:
                b.w = [o for o in b.w if o.dma or o.eng != eng]
            b.w.append(op)
        self.ops[eng].append(op)
        if dma:
            self.all_dma.append(op)
        return op

    def dma(self, out, in_, reads=(), writes=(), pwrites=(), q="sp"):
        def fn(e):
            return e.dma_start(out=out, in_=in_)
        return self.add(q, fn, reads, writes, pwrites, dma=True)

    def mm(self, out, lhsT, rhs, start, stop, reads=(), writes=(), pwrites=(), skip=False):
        def fn(e):
            if skip:
                return e.matmul(out, lhsT, rhs, start=start, stop=stop, skip_group_check=True)
            return e.matmul(out, lhsT, rhs, start=start, stop=stop)
        return self.add("pe", fn, reads, writes, pwrites)

    def tr(self, out, in_, ident, reads=(), writes=(), pwrites=()):
        def fn(e):
            return e.transpose(out, in_, ident)
        return self.add("pe", fn, reads, writes, pwrites)

    def act(self, out, in_, func, bias=None, scale=None, accum_out=None,
            reads=(), writes=(), pwrites=()):
        kw = {}
        if bias is not None:
            kw["bias"] = bias
        if scale is not None:
            kw["scale"] = scale
        if accum_out is not None:
            kw["accum_out"] = accum_out

        def fn(e):
            return e.activation(out=out, in_=in_, func=func, **kw)
        return self.add("act", fn, reads, writes, pwrites)

    def tt(self, eng, out, in0, in1, op, reads=(), writes=(), pwrites=()):
        def fn(e):
            return e.tensor_tensor(out=out, in0=in0, in1=in1, op=op)
        return self.add(eng, fn, reads, writes, pwrites)

    def ts(self, eng, out, in0, s1, s2, op0, op1=None, reads=(), writes=(), pwrites=()):
        def fn(e):
            if op1 is None:
                return e.tensor_scalar(out=out, in0=in0, scalar1=s1, scalar2=None, op0=op0)
            return e.tensor_scalar(out=out, in0=in0, scalar1=s1, scalar2=s2, op0=op0, op1=op1)
        return self.add(eng, fn, reads, writes, pwrites)

    def stt(self, out, in0, scalar, in1, op0, op1, reads=(), writes=(), pwrites=()):
        def fn(e):
            return e.scalar_tensor_tensor(out=out, in0=in0, scalar=scalar, in1=in1,
                                          op0=op0, op1=op1)
        return self.add("dve", fn, reads, writes, pwrites)

    def copy(self, eng, out, in_, reads=(), writes=(), pwrites=()):
        if eng == "act":
            return self.act(out, in_, AF.Copy, reads=reads, writes=writes, pwrites=pwrites)

        def fn(e):
            return e.tensor_copy(out=out, in_=in_)
        return self.add(eng, fn, reads, writes, pwrites)

    def recip(self, out, in_, reads=(), writes=(), pwrites=()):
        def fn(e):
            return e.reciprocal(out=out, in_=in_)
        return self.add("dve", fn, reads, writes, pwrites)

    def memset(self, eng, ap, val, writes=(), pwrites=()):
        def fn(e):
            return e.memset(ap, val)
        return self.add(eng, fn, (), writes, pwrites)

    def emit(self, final=False):
        nc = self.nc
        for e in self.ENGS:
            for op in self.ops[e]:
                for d in op.deps:
                    if d.eng == "pe" and op.eng == "pe" and not d.dma:
                        continue
                    d.sig = True
        for e in self.ENGS:
            lst = [o for o in self.ops[e] if not o.dma]
            if lst:
                lst[-1].sig = True
        for op in self.all_dma:
            op.sig = True
        for e in self.ENGS:
            for op in self.ops[e]:
                if not op.sig:
                    continue
                if op.dma:
                    k = self.dnext % NDMASEM
                    self.dnext += 1
                    self.dcnt[k] += 16
                    op.sem = self.dsem[k]
                    op.val = self.dcnt[k]
                    op.idx = ("d", k)
                else:
                    self.ecnt[e] += 1
                    op.sem = self.esem[e]
                    op.val = self.ecnt[e]
                    op.idx = ("e", e)
        prev_on = {}
        for op in sorted(self.all_dma, key=lambda o: o.seq):
            if op.idx in prev_on:
                op.deps.append(prev_on[op.idx])
            prev_on[op.idx] = op
        targets = []
        for e in self.ENGS:
            if self.ecnt[e] > 0:
                targets.append((("e", e), self.esem[e], self.ecnt[e]))
        for k in range(NDMASEM):
            if self.dcnt[k] > 0:
                targets.append((("d", k), self.dsem[k], self.dcnt[k]))
        ops = self.ops
        waited = self.waited
        allops = sorted((o for e in self.ENGS for o in self.ops[e]), key=lambda o: o.seq)
        evc = self.evc
        for op in allops:
            if op.dma:
                vc = {}
            else:
                vc = evc[op.eng]
            for d in op.deps:
                for k, v in d.vc.items():
                    if vc.get(k, 0) < v:
                        vc[k] = v
            if op.sig:
                vc[op.idx] = op.val
            if op.dma:
                op.vc = vc
            else:
                op.vc = dict(vc)

        def run(ename, eng):
            wd = waited[ename]
            for op in ops[ename]:
                need = {}
                for d in op.deps:
                    if d.eng == "pe" and ename == "pe" and not d.dma:
                        continue
                    if wd.get(d.idx, 0) >= d.val:
                        continue
                    if d.idx not in need or need[d.idx].val < d.val:
                        need[d.idx] = d
                rem = list(need.values())
                keep = []
                for d in rem:
                    cov = False
                    for o in rem:
                        if o is not d and o.vc.get(d.idx, 0) >= d.val:
                            cov = True
                            break
                    if not cov:
                        keep.append(d)
                for d in keep:
                    for k, v in d.vc.items():
                        if wd.get(k, 0) < v:
                            wd[k] = v
                wl = [(d.sem, d.val) for d in keep]
                for sem, val in wl[:-1]:
                    eng.wait_ge(sem, val)
                inst = op.fn(eng)
                if wl:
                    inst._wait_ge(wl[-1][0], wl[-1][1])
                if op.sig:
                    inst.then_inc(op.sem, 16 if op.dma else 1)
            for key, sem, val in targets:
                if wd.get(key, 0) < val:
                    eng.wait_ge(sem, val)
                    wd[key] = val

        with nc.Block() as block:
            @block.tensor
            def _(eng):
                run("pe", eng)

            @block.scalar
            def _(eng):
                run("act", eng)

            @block.vector
            def _(eng):
                run("dve", eng)

            @block.gpsimd
            def _(eng):
                run("pool", eng)

            @block.sync
            def _(eng):
                run("sp", eng)
        self.ops = {e: [] for e in self.ENGS}
        self.all_dma = []


def tiles_for(parity, nslot):
    res = []
    for j in range(nslot):
        lo, hi = 2 * j, 2 * j + 1
        a, b = (lo, hi) if j % 2 == 0 else (hi, lo)
        res.append(a if parity == 0 else b)
    return res


def bias_index(nslot):
    idx = {}
    n = 0
    for h in range(4):
        for j in range(nslot):
            for kb in range(8 * j + 8):
                idx[(h, j, kb)] = n
                n += 1
    return idx, n


def build_program(SEQ, phases="A1,A2,B,C", debug=False):
    SQ = SEQ // 2
    NT = SEQ // 512
    NSLOT = SQ // 512
    NKB = SEQ // 128
    bidx, NBIAS = bias_index(NSLOT)
    phases = phases.split(",")

    nc = bass.Bass("TRN2", target_bir_lowering=False)
    dk = "ExternalOutput" if debug else "Internal"

    def din(name, shape, dt):
        return nc.dram_tensor(name, shape, dt, kind="ExternalInput").ap()

    def dscr(name, shape, dt):
        if debug:
            return nc.dram_tensor(name, shape, dt, kind="ExternalOutput").ap()
        return nc.dram_tensor(name, shape, dt).ap()

    xk = din("xk", [SEQ, D], F32)
    xq = din("xq", [SQ, D], F32)
    w_in = din("w_in", [D, 6144], F32)
    w_o_sb = din("w_o_sb", [512, D], F32)
    w_o_df = din("w_o_df", [512, D], F32)
    w_out = din("w_out", [D, D], F32)
    pre_g = din("pre_g", [128, D], F32)
    post_g = din("post_g", [128, D], F32)
    bgate = din("bgate", [128, 16], F32)
    subln = din("subln", [128, 1], F32)
    lam_in = din("lam_in", [128, 4 * HD], F32)
    ident_d = din("ident", [128, 128], BF16)
    negtri_d = din("negtri", [128, 256], BF16)
    ones_d = din("ones", [128, 128], BF16)
    onesf_d = din("onesf", [128, 128], F32)
    strips_d = din("strips", [128, 4 * STRIP_W], BF16)
    bias_d = din("biastab", [128, NBIAS], F32)
    out_d = nc.dram_tensor("out", [SQ, D], F32, kind="ExternalOutput").ap()

    KT = dscr("KT", [1024, SEQ], BF16)
    Vg = dscr("Vg", [8, 128, NKB, 128], BF16)
    QTs = dscr("QTs", [NSLOT, 128, 8, 512], BF16)
    ZTs = dscr("ZTs", [NSLOT, 128, 8, 512], BF16)
    GTs = dscr("GTs", [NSLOT, 128, 16, 512], BF16)
    OTs = dscr("OTs", [NSLOT, 128, 8, 512], BF16)

    uid = [0]
    with ExitStack() as es0:
        S = Sched(nc, es0)

        def phase_A(which):
            with ExitStack() as es:
                def sb(name, shape, dt):
                    uid[0] += 1
                    nm = "sb%d_%s" % (uid[0], name)
                    return T(es.enter_context(nc.sbuf_tensor(nm, shape, dt)), nm)

                def ps(name, shape, dt):
                    uid[0] += 1
                    nm = "ps%d_%s" % (uid[0], name)
                    return T(es.enter_context(nc.psum_tensor(nm, shape, dt)), nm)

                ident = sb("ident", [128, 128], BF16)
                S.dma(ident.t[:], ident_d, writes=[ident])
                gpre = sb("gpre", [128, D], F32)
                S.dma(gpre.t[:], pre_g, writes=[gpre])
                xt = [sb("xt%d" % i, [128, D], F32) for i in range(3)]
                junk = sb("junk", [128, D], BF16)
                ss = [sb("ss%d" % i, [128, 1], F32) for i in range(3)]
                lnv = [sb("lnv%d" % i, [128, 1], F32) for i in range(3)]
                rstd = [sb("rstd%d" % i, [128, 1], F32) for i in range(3)]
                hb = [sb("hb%d" % i, [128, D], BF16) for i in range(2)]
                hT = [sb("hT%d" % i, [128, 8, 512], BF16) for i in range(2)]
                stg = [sb("stg%d" % i, [128, 8, 512], F32) for i in range(2)]
                tp = [ps("tp%d" % i, [128, 1024], BF16) for i in range(2)]
                acc = [ps("pacc%d" % i, [128, 512], F32) for i in range(5)]
                cnt = {"stg": 0, "blk": 0, "acc": 0, "ev": 0}

                def load_w(dst, dcol, src, scol, nch):
                    st = stg[cnt["stg"] % 2]
                    cnt["stg"] += 1
                    S.dma(st.t[:, 0:nch, :],
                          src[:, scol:scol + 512].rearrange("(c p) n -> p c n", p=128),
                          writes=[st])
                    S.copy("dve", dst.t[:, :, dcol:dcol + 512], st.t[:, 0:nch, :],
                           reads=[st], pwrites=[dst])

                if which == 1:
                    xsrc, ntile = xk, NT
                    wk = sb("wk", [128, 8, 1024], BF16)
                    wv = sb("wv", [128, 8, 1024], BF16)
                    load_w(wk, 0, w_in, 512, 8)
                    load_w(wk, 512, w_in, 2560, 8)
                    load_w(wv, 0, w_in, 1024, 8)
                    load_w(wv, 512, w_in, 3072, 8)
                    kst = [sb("kst%d" % i, [128, 8, 512], BF16) for i in range(2)]
                    vst = [sb("vst%d" % i, [128, 8, 4, 128], BF16) for i in range(2)]
                else:
                    xsrc, ntile = xq, NSLOT
                    wq = sb("wq", [128, 8, 2048], BF16)
                    wg = sb("wg", [128, 8, 2048], BF16)
                    load_w(wq, 0, w_in, 0, 8)
                    load_w(wq, 512, w_in, 2048, 8)
                    load_w(wq, 1024, w_in, 1536, 8)
                    load_w(wq, 1536, w_in, 3584, 8)
                    for i in range(4):
                        load_w(wg, 512 * i, w_in, 4096 + 512 * i, 8)
                    bg = sb("bg", [128, 16], F32)
                    S.dma(bg.t[:], bgate, writes=[bg])
                    qst = [sb("qst%d" % i, [128, 8, 512], BF16) for i in range(2)]

                nst = {}

                def normA(tt, sub):
                    i = cnt["blk"]
                    cnt["blk"] += 1
                    nst[(tt, sub)] = i
                    x_ = xt[i % 3]
                    r0 = tt * 512 + sub * 128
                    S.dma(x_.t[:], xsrc[r0:r0 + 128, :], writes=[x_])
                    s_, l_, r_ = ss[i % 3], lnv[i % 3], rstd[i % 3]
                    S.act(junk.t[:], x_.t[:], AF.Square, accum_out=s_.t[:],
                          reads=[x_], writes=[junk, s_])
                    S.act(l_.t[:], s_.t[:], AF.Ln, bias=EPS, scale=1.0 / D,
                          reads=[s_], writes=[l_])
                    S.act(r_.t[:], l_.t[:], AF.Exp, scale=-0.5, reads=[l_], writes=[r_])
                    hb_ = hb[i % 2]
                    S.stt(hb_.t[:], x_.t[:], r_.t[:], gpre.t[:], ALU.mult, ALU.mult,
                          reads=[x_, r_, gpre], writes=[hb_])

                def normB(tt, sub):
                    i = nst[(tt, sub)]
                    h = hT[tt % 2]
                    hb_ = hb[i % 2]
                    tp_ = tp[i % 2]
                    for c in range(8):
                        S.tr(tp_.t[:, c * 128:(c + 1) * 128], hb_.t[:, c * 128:(c + 1) * 128],
                             ident.t[:], reads=[hb_, ident], pwrites=[tp_])
                    S.copy("dve" if sub % 2 == 0 else "act",
                           h.t[:, :, sub * 128:(sub + 1) * 128],
                           tp_.t[:].rearrange("p (c n) -> p c n", c=8),
                           reads=[tp_], pwrites=[h])

                def evac(eng, out, in_, func, bias, scale, reads, pwrites):
                    if func is None and eng == "dve":
                        if scale is None:
                            S.copy("dve", out, in_, reads=reads, pwrites=pwrites)
                        else:
                            S.ts("dve", out, in_, scale, None, ALU.mult, reads=reads,
                                 pwrites=pwrites)
                    else:
                        S.act(out, in_, func if func is not None else AF.Copy, bias=bias,
                              scale=scale, reads=reads, pwrites=pwrites)

                def proj1(tt):
                    h = hT[tt % 2]
                    k_ = kst[tt % 2]
                    v_ = vst[tt % 2]
                    for kc in range(8):
                        a_ = acc[cnt["acc"] % 5]
                        cnt["acc"] += 1
                        for c in range(8):
                            S.mm(a_.t[:], wk.t[:, c, kc * 128:(kc + 1) * 128], h.t[:, c, :],
                                 c == 0, c == 7, reads=[wk, h], pwrites=[a_])
                        eng = "dve" if cnt["ev"] % 2 == 0 else "act"
                        cnt["ev"] += 1
                        evac(eng, k_.t[:, kc, :], a_.t[:], None, None, None, [a_], [k_])
                        yield
                    S.dma(KT[:, tt * 512:(tt + 1) * 512].rearrange("(c p) n -> p c n", p=128),
                          k_.t[:], reads=[k_])
                    for sub in range(4):
                        for slab in range(2):
                            a_ = acc[cnt["acc"] % 5]
                            cnt["acc"] += 1
                            for c in range(8):
                                S.mm(a_.t[:], h.t[:, c, sub * 128:(sub + 1) * 128],
                                     wv.t[:, c, slab * 512:(slab + 1) * 512],
                                     c == 0, c == 7, reads=[wv, h], pwrites=[a_])
                            eng = "dve" if cnt["ev"] % 2 == 0 else "act"
                            cnt["ev"] += 1
                            evac(eng, v_.t[:, slab * 4:(slab + 1) * 4, sub, :],
                                 a_.t[:].rearrange("p (g n) -> p g n", g=4),
                                 None, None, None, [a_], [v_])
                            yield
                    S.dma(Vg[:, :, tt * 4:(tt + 1) * 4, :].rearrange("g p s n -> p g s n"),
                          v_.t[:], reads=[v_])

                def proj2(tt):
                    h = hT[tt % 2]
                    for grp in range(4):
                        q_ = qst[(tt * 4 + grp) % 2]
                        for oc8 in range(8):
                            oc = grp * 8 + oc8
                            a_ = acc[cnt["acc"] % 5]
                            cnt["acc"] += 1
                            wsrc, wc = (wq, oc) if oc < 16 else (wg, oc - 16)
                            for c in range(8):
                                S.mm(a_.t[:], wsrc.t[:, c, wc * 128:(wc + 1) * 128], h.t[:, c, :],
                                     c == 0, c == 7, reads=[wsrc, h], pwrites=[a_])
                            if grp == 0:
                                eng = "dve" if oc8 % 2 == 0 else "act"
                                evac(eng, q_.t[:, oc8, :], a_.t[:], None, None, 0.125, [a_], [q_])
                            elif grp == 1:
                                evac("act", q_.t[:, oc8, :], a_.t[:], AF.Silu, None, None,
                                     [a_], [q_])
                            else:
                                gc = oc - 16
                                evac("act", q_.t[:, oc8, :], a_.t[:], AF.Sigmoid,
                                     bg.t[:, gc:gc + 1], None, [a_, bg], [q_])
                            yield
                        dst = (QTs[tt], ZTs[tt], GTs[tt, :, 0:8, :], GTs[tt, :, 8:16, :])[grp]
                        S.dma(dst, q_.t[:], reads=[q_])

                proj = proj1 if which == 1 else proj2
                G = 16 if which == 1 else 32
                marks = {1 + (G * s_) // 4: s_ for s_ in range(4)}
                for sub in range(4):
                    normA(0, sub)
                    normB(0, sub)
                for tt in range(ntile):
                    gi = 0
                    nxt = tt + 1 < ntile
                    for _ in proj(tt):
                        gi += 1
                        if nxt and gi in marks:
                            sub = marks[gi]
                            normA(tt + 1, sub)
                            if sub > 0:
                                normB(tt + 1, sub - 1)
                    if nxt:
                        normB(tt + 1, 3)
                S.emit()

        if "A1" in phases:
            phase_A(1)
        if "A2" in phases:
            phase_A(2)

        def phase_B2():
            from collections import deque
            with ExitStack() as es:
                def sb(name, shape, dt):
                    uid[0] += 1
                    nm = "sb%d_%s" % (uid[0], name)
                    return T(es.enter_context(nc.sbuf_tensor(nm, shape, dt)), nm)

                def ps(name, shape, dt):
                    uid[0] += 1
                    nm = "ps%d_%s" % (uid[0], name)
                    return T(es.enter_context(nc.psum_tensor(nm, shape, dt)), nm)

                negtri = sb("negtri", [128, 256], BF16)
                S.dma(negtri.t[:], negtri_d, writes=[negtri])
                ones = sb("ones", [128, 128], BF16)
                S.dma(ones.t[:], ones_d, writes=[ones])
                onesf = sb("onesf", [128, 128], F32)
                S.dma(onesf.t[:], onesf_d, writes=[onesf])
                strips = sb("strips", [128, 4 * STRIP_W], BF16)
                S.dma(strips.t[:], strips_d, writes=[strips])
                biast = sb("biast", [128, NBIAS], F32)
                S.dma(biast.t[:], bias_d, writes=[biast])
                sublc = sb("sublc", [128, 1], F32)
                S.dma(sublc.t[:], subln, writes=[sublc])
                lamv = sb("lamv", [128, 4 * HD], F32)
                S.dma(lamv.t[:], lam_in, writes=[lamv])
                lprod = sb("lprod", [128, 2 * HD], F32)
                lsum = sb("lsum", [128, 2], F32)
                lexp = sb("lexp", [128, 2], F32)
                neglam = sb("neglam", [128, 1], F32)
                gcol = sb("gcol", [128, 1], F32)
                S.tt("dve", lprod.t[:, 0:HD], lamv.t[:, 0:HD], lamv.t[:, HD:2 * HD], ALU.mult,
                     reads=[lamv], pwrites=[lprod])
                S.tt("dve", lprod.t[:, HD:2 * HD], lamv.t[:, 2 * HD:3 * HD], lamv.t[:, 3 * HD:4 * HD],
                     ALU.mult, reads=[lamv], pwrites=[lprod])

                def red(e):
                    return e.tensor_reduce(out=lsum.t[:], in_=lprod.t[:].rearrange("p (a b) -> p a b", a=2),
                                           axis=mybir.AxisListType.X, op=ALU.add)
                S.add("dve", red, reads=[lprod], writes=[lsum])
                S.act(lexp.t[:], lsum.t[:], AF.Exp, reads=[lsum], writes=[lexp])
                S.stt(neglam.t[:], lexp.t[:, 1:2], -LAMBDA_INIT, lexp.t[:, 0:1], ALU.add, ALU.subtract,
                      reads=[lexp], writes=[neglam])
                S.ts("dve", gcol.t[:], sublc.t[:], 1.0 - LAMBDA_INIT, None, ALU.mult,
                     reads=[sublc], writes=[gcol])

                NCH = 4
                CH = NKB // NCH
                kTs = sb("kTs", [128, SEQ], BF16)
                vvs = sb("vvs", [128, NKB, 128], BF16)
                kTd = sb("kTd", [128, SEQ], BF16)
                vvd = sb("vvd", [128, NKB, 128], BF16)
                kTs_b = [Buf() for _ in range(NCH)]
                vvs_b = [Buf() for _ in range(NCH)]
                kTd_b = [Buf() for _ in range(NCH)]
                vvd_b = [Buf() for _ in range(NCH)]
                qs_ = [sb("qs%d" % i, [128, 512], BF16) for i in range(3)]
                zs_ = [sb("zs%d" % i, [128, 512], BF16) for i in range(3)]
                qd_ = [sb("qd%d" % i, [128, 512], BF16) for i in range(3)]
                zd_ = [sb("zd%d" % i, [128, 512], BF16) for i in range(3)]
                os_ = [sb("os%d" % i, [128, 512], BF16) for i in range(2)]
                od_ = [sb("od%d" % i, [128, 512], BF16) for i in range(2)]
                NW = 4
                e_t = [sb("e%d" % i, [128, 512], F32) for i in range(NW)]
                sp_t = [sb("sp%d" % i, [128, 512], BF16) for i in range(NW)]
                acc_t = [sb("ac%d" % i, [128, 512], BF16) for i in range(NW)]
                a_t = [sb("a%d" % i, [128, 512], BF16) for i in range(NW)]
                NP = 3
                p_t = [sb("p%d" % i, [128, 512], BF16) for i in range(NP)]
                NZS = 3
                zsb = [ps("zsb%d" % i, [128, 512], F32) for i in range(NZS)]
                NZD = 1
                zdf = [ps("zdf%d" % i, [128, 512], F32) for i in range(NZD)]
                subps = ps("subps", [128, 512], F32)
                obs = ps("obs", [128, 512], F32)
                obs_b = [Buf(), Buf()]
                obd = ps("obd", [128, 512], F32)
                lbd = ps("lbd", [128, 512], F32)
                uraw = [[sb("uraw%d_%d" % (i, m), [128, 512], F32) for m in range(2)] for i in range(2)]
                lraw = [[sb("lraw%d_%d" % (i, m), [128, 512], F32) for m in range(2)] for i in range(2)]
                r_t = [sb("r%d" % i, [128, 512], F32) for i in range(2)]
                u_t = [sb("u%d" % i, [128, 512], F32) for i in range(3)]
                cnt = {"zs": 0, "zd": 0, "w": 0, "p": 0}
                bg = deque()
                for t_ in sp_t + a_t + p_t:
                    S.memset("pool", t_.t[:], 0.0, writes=[t_])

                def ucols(kb, nkb):
                    i = kb - (nkb - 4)
                    return slice(128 * i, 512) if i > 0 else slice(0, 512)

                def strip(which_set, which_sb, sub, nonstrict):
                    k = which_set * 2 + which_sb
                    off = k * STRIP_W + 384 - 128 * sub + (1 if nonstrict else 0)
                    return strips.t[:, off:off + 512]

                def load_pair(g):
                    gd = 4 + g
                    for c in range(NCH - 1, -1, -1):
                        ks = slice(c * CH * 128, (c + 1) * CH * 128)
                        S.dma(kTs.t[:, ks], KT[g * 128:(g + 1) * 128, ks], writes=[kTs_b[c]], q="sp")
                        S.dma(vvs.t[:, c * CH:(c + 1) * CH, :], Vg[g, :, c * CH:(c + 1) * CH, :],
                              writes=[vvs_b[c]], q="sp")
                        S.dma(kTd.t[:, ks], KT[gd * 128:(gd + 1) * 128, ks], writes=[kTd_b[c]], q="sp")
                        S.dma(vvd.t[:, c * CH:(c + 1) * CH, :], Vg[gd, :, c * CH:(c + 1) * CH, :],
                              writes=[vvd_b[c]], q="sp")

                def load_slot(g, j):
                    gd = 4 + g
                    i = j % 3
                    cs = slice(j * 512, (j + 1) * 512)
                    S.dma(qs_[i].t[:], QTs[j, :, g, :], writes=[qs_[i]])
                    S.dma(zs_[i].t[:], ZTs[j, :, g, :], writes=[zs_[i]])
                    S.dma(qd_[i].t[:], QTs[j, :, gd, :], writes=[qd_[i]])
                    S.dma(zd_[i].t[:], ZTs[j, :, gd, :], writes=[zd_[i]])

                def run_pair(g):
                    gd = 4 + g
                    units = []
                    order = list(range(NSLOT - 1, -1, -1))
                    for j in order:
                        nkb = 8 * j + 8
                        for half in range(2):
                            for n in range(nkb):
                                units.append((j, half, n, nkb))
                    NU = len(units)
                    st = [None] * NU
                    load_slot(g, order[0])
                    d_cut = 120.0 / SLOPES[g]

                    def kbmin(j):
                        return max(0, int((1024 * j - d_cut) // 128))

                    def s0(u):
                        j, half, n, nkb = units[u]
                        if n == 0 and half == 0:
                            oi = order.index(j)
                            if oi + 1 < len(order):
                                load_slot(g, order[oi + 1])
                        kb = nkb - 1 - n
                        sl = j % 2
                        prt = slice(half * 64, (half + 1) * 64)
                        z = zsb[cnt["zs"] % NZS]
                        cnt["zs"] += 1
                        w = cnt["w"] % NW
                        cnt["w"] += 1
                        st[u] = {"z": z, "w": w}
                        S.mm(z.t[:], kTs.t[prt, kb * 128:(kb + 1) * 128], qs_[j % 3].t[prt, :],
                             True, True, reads=[kTs_b[kb // CH], qs_[j % 3]], pwrites=[z])

                    def s1(u):
                        j, half, n, nkb = units[u]
                        cs = ucols(nkb - 1 - n, nkb)
                        z, w = st[u]["z"], st[u]["w"]
                        S.act(e_t[w].t[:, cs], z.t[:, cs], AF.Exp, reads=[z], writes=[e_t[w]])

                    def s2(u):
                        j, half, n, nkb = units[u]
                        kb = nkb - 1 - n
                        z, w = st[u]["z"], st[u]["w"]
                        cs = ucols(kb, nkb)
                        S.act(sp_t[w].t[:, cs], e_t[w].t[:, cs], AF.Ln, bias=1.0, scale=1.0,
                              reads=[e_t[w]], writes=[sp_t[w]])
                        if kb >= nkb - 8:
                            rel = kb - (nkb - 8)
                            m = strip(j % 2, rel // 4, rel % 4, False)
                            S.tt("dve", sp_t[w].t[:], sp_t[w].t[:], m, ALU.mult,
                                 reads=[strips, sp_t[w]], writes=[sp_t[w]])

                    def s3(u):
                        j, half, n, nkb = units[u]
                        z, w = st[u]["z"], st[u]["w"]
                        S.mm(z.t[:], negtri.t[:, 0:128], sp_t[w].t[:], False, n == 0,
                             reads=[negtri, sp_t[w]], pwrites=[z], skip=True)
                        if n > 0:
                            pw = st[u - 1]["w"]
                            S.mm(z.t[:], negtri.t[:, 128:256], acc_t[pw].t[:], False, True,
                                 reads=[negtri, acc_t[pw]], pwrites=[z], skip=True)
                        if n + 1 < nkb:
                            if n == 0:
                                S.copy("dve", acc_t[w].t[:], sp_t[w].t[:], reads=[sp_t[w]],
                                       writes=[acc_t[w]])
                            else:
                                pw = st[u - 1]["w"]
                                S.tt("dve", acc_t[w].t[:], acc_t[pw].t[:], sp_t[w].t[:], ALU.add,
                                     reads=[acc_t[pw], sp_t[w]], writes=[acc_t[w]])

                    def s4(u):
                        j, half, n, nkb = units[u]
                        kb = nkb - 1 - n
                        z, w = st[u]["z"], st[u]["w"]
                        cs = ucols(kb, nkb)
                        S.act(a_t[w].t[:, cs], z.t[:, cs], AF.Exp, reads=[z], writes=[a_t[w]])
                        if kb >= nkb - 8:
                            rel = kb - (nkb - 8)
                            m = strip(j % 2, rel // 4, rel % 4, False)
                            S.tt("dve", a_t[w].t[:], a_t[w].t[:], m, ALU.mult,
                                 reads=[strips, a_t[w]], writes=[a_t[w]])

                    def s5(u):
                        j, half, n, nkb = units[u]
                        kb = nkb - 1 - n
                        sl = j % 2
                        prt = slice(half * 64, (half + 1) * 64)
                        w = st[u]["w"]
                        S.mm(obs.t[prt, :], vvs.t[:, kb, half * 64:(half + 1) * 64], a_t[w].t[:],
                             n == 0, n == nkb - 1, reads=[vvs_b[kb // CH], a_t[w]], pwrites=[obs_b[half]])
                        if n == nkb - 1:
                            S.tt("dve", os_[sl].t[prt, :], obs.t[prt, :], zs_[j % 3].t[prt, :], ALU.mult,
                                 reads=[obs_b[half], zs_[j % 3]], pwrites=[os_[sl]])
                            if half == 1:
                                S.dma(OTs[j, :, g, :], os_[sl].t[:], reads=[os_[sl]])

                    def d0(u):
                        j, half, n, nkb = units[u]
                        if n < kbmin(j):
                            return
                        kb = n
                        sl = j % 2
                        prt = slice((1 - half) * 64, (2 - half) * 64)
                        zd = zdf[cnt["zd"] % NZD]
                        cnt["zd"] += 1
                        wp = cnt["p"] % NP
                        cnt["p"] += 1
                        st[u]["zd"] = zd
                        st[u]["wp"] = wp
                        S.mm(zd.t[:], kTd.t[prt, kb * 128:(kb + 1) * 128], qd_[j % 3].t[prt, :],
                             True, True, reads=[kTd_b[kb // CH], qd_[j % 3]], pwrites=[zd])

                    def d1(u):
                        j, half, n, nkb = units[u]
                        if n < kbmin(j):
                            return
                        kb = n
                        zd, wp = st[u]["zd"], st[u]["wp"]
                        bc = bidx[(g, j, kb)]
                        cs = ucols(kb, nkb)
                        S.act(p_t[wp].t[:, cs], zd.t[:, cs], AF.Exp, reads=[zd], writes=[p_t[wp]])
                        if kb >= nkb - 8:
                            rel = kb - (nkb - 8)
                            mk = strip(j % 2, rel // 4, rel % 4, True)
                            S.stt(p_t[wp].t[:], p_t[wp].t[:], biast.t[:, bc:bc + 1], mk, ALU.mult, ALU.mult,
                                  reads=[strips, biast, p_t[wp]], writes=[p_t[wp]])
                        else:
                            S.ts("dve", p_t[wp].t[:], p_t[wp].t[:], biast.t[:, bc:bc + 1], None, ALU.mult,
                                 reads=[biast, p_t[wp]], writes=[p_t[wp]])

                    def slot_epilogue(j):
                        sl = j % 2
                        ur, lr = uraw[sl], lraw[sl]
                        ops = []
                        for m in range(2):
                            for q4 in range(8):
                                cs = slice(q4 * 64, (q4 + 1) * 64)
                                ops.append(lambda m=m, cs=cs: S.recip(r_t[m].t[:, cs], lr[m].t[:, cs],
                                                                     reads=[lr[m]], pwrites=[r_t[m]]))
                            ops.append(lambda m=m: S.tt("dve", u_t[m].t[:], ur[m].t[:], r_t[m].t[:], ALU.mult,
                                                        reads=[ur[m], r_t[m]], writes=[u_t[m]]))
                        ops.append(lambda: S.stt(u_t[2].t[:], u_t[1].t[:], neglam.t[:], u_t[0].t[:],
                                                 ALU.mult, ALU.add,
                                                 reads=[u_t[0], u_t[1], neglam], writes=[u_t[2]]))

                        def tail():
                            S.act(r_t[0].t[:], u_t[2].t[:], AF.Square, reads=[u_t[2]], writes=[r_t[0]])
                            zq = subps
                            S.mm(zq.t[:], onesf.t[:], r_t[0].t[:], True, True, reads=[onesf, r_t[0]],
                                 pwrites=[zq])
                            S.act(r_t[1].t[:], zq.t[:], AF.Ln, bias=EPS, scale=1.0, reads=[zq],
                                  writes=[r_t[1]])
                            S.act(r_t[0].t[:], r_t[1].t[:], AF.Exp, scale=-0.5, reads=[r_t[1]],
                                  writes=[r_t[0]])
                            S.tt("dve", u_t[0].t[:], u_t[2].t[:], r_t[0].t[:], ALU.mult,
                                 reads=[u_t[2], r_t[0]], writes=[u_t[0]])
                            S.stt(od_[sl].t[:], u_t[0].t[:], gcol.t[:], zd_[j % 3].t[:], ALU.mult, ALU.mult,
                                  reads=[u_t[0], gcol, zd_[j % 3]], writes=[od_[sl]])
                            S.dma(OTs[j, :, gd, :], od_[sl].t[:], reads=[od_[sl]])
                        ops.append(tail)
                        bg.extend(ops)

                    def d2(u):
                        j, half, n, nkb = units[u]
                        if n < kbmin(j):
                            return
                        kb = n
                        sl = j % 2
                        wp = st[u]["wp"]
                        first = (n == kbmin(j))
                        S.mm(obd.t[:], vvd.t[:, kb, :], p_t[wp].t[:], first, n == nkb - 1,
                             reads=[vvd_b[kb // CH], p_t[wp]], pwrites=[obd])
                        S.mm(lbd.t[:], ones.t[:], p_t[wp].t[:], first, n == nkb - 1,
                             reads=[ones, p_t[wp]], pwrites=[lbd])
                        if n == nkb - 1:
                            S.copy("dve", uraw[sl][1 - half].t[:], obd.t[:], reads=[obd],
                                   writes=[uraw[sl][1 - half]])
                            S.copy("dve", lraw[sl][1 - half].t[:], lbd.t[:], reads=[lbd],
                                   writes=[lraw[sl][1 - half]])
                            if half == 1:
                                slot_epilogue(j)

                    sched = [(s5, 4), (s4, 3), (s3, 2), (d2, 2), (s1, 1), (d1, 1), (s2, 1), (s0, 0), (d0, 0)]
                    for step in range(NU + 5):
                        for fn, sk in sched:
                            u = step - sk
                            if 0 <= u < NU:
                                fn(u)
                        for _ in range(1):
                            if bg:
                                bg.popleft()()
                    while bg:
                        bg.popleft()()

                load_pair(0)
                for g in range(4):
                    run_pair(g)
                    if g + 1 < 4:
                        load_pair(g + 1)
                S.emit()

        if "B" in phases:
            phase_B2()

        def phase_C():
            with ExitStack() as es:
                def sb(name, shape, dt):
                    uid[0] += 1
                    nm = "sb%d_%s" % (uid[0], name)
                    return T(es.enter_context(nc.sbuf_tensor(nm, shape, dt)), nm)

                def ps(name, shape, dt):
                    uid[0] += 1
                    nm = "ps%d_%s" % (uid[0], name)
                    return T(es.enter_context(nc.psum_tensor(nm, shape, dt)), nm)

                stg = [sb("stgc%d" % i, [128, 8, 512], F32) for i in range(2)]
                wosb = sb("wosb", [128, 4, 1024], BF16)
                wodf = sb("wodf", [128, 4, 1024], BF16)
                wout = sb("wout", [128, 8, 1024], BF16)
                pg = sb("pg", [128, D], F32)
                S.dma(pg.t[:], post_g, writes=[pg])
                n = 0
                for dst, src, nch in ((wosb, w_o_sb, 4), (wodf, w_o_df, 4), (wout, w_out, 8)):
                    for half in range(2):
                        st = stg[n % 2]
                        n += 1
                        S.dma(st.t[:, 0:nch, :],
                              src[:, half * 512:(half + 1) * 512].rearrange("(c p) n -> p c n", p=128),
                              writes=[st])
                        S.copy("dve", dst.t[:, :, half * 512:(half + 1) * 512], st.t[:, 0:nch, :],
                               reads=[st], pwrites=[dst])
                oTt = [sb("oTt%d" % i, [128, 8, 512], BF16) for i in range(2)]
                Gt = [sb("Gt%d" % i, [128, 16, 512], BF16) for i in range(2)]
                xr = [sb("xr%d" % i, [128, 4, D], F32) for i in range(2)]
                m1 = [sb("m1_%d" % i, [128, 512], F32) for i in range(2)]
                m2 = [sb("m2_%d" % i, [128, 512], F32) for i in range(2)]
                mg = [sb("mg%d" % i, [128, 8, 512], BF16) for i in range(2)]
                ot = [sb("ot%d" % i, [128, D], F32) for i in range(2)]
                t1 = [sb("t1_%d" % i, [128, D], F32) for i in range(2)]
                junk = sb("junkc", [128, 512], BF16)
                ssa = [sb("ssa%d" % i, [128, 2], F32) for i in range(2)]
                sst = [sb("sst%d" % i, [128, 1], F32) for i in range(2)]
                lnv = [sb("lnvc%d" % i, [128, 1], F32) for i in range(2)]
                rstd = [sb("rstdc%d" % i, [128, 1], F32) for i in range(2)]
                py = [ps("py%d" % i, [128, 512], F32) for i in range(4)]
                po = [ps("po%d" % i, [128, 512], F32) for i in range(4)]
                cnt = {"y": 0, "o": 0, "m": 0, "s": 0}

                def load(j):
                    i = j % 2
                    cs = slice(j * 512, (j + 1) * 512)
                    S.dma(oTt[i].t[:], OTs[j], writes=[oTt[i]])
                    S.dma(Gt[i].t[:], GTs[j], writes=[Gt[i]])
                    S.dma(xr[i].t[:], xq[cs, :].rearrange("(s p) n -> p s n", p=128), writes=[xr[i]])

                def tile(j):
                    i = j % 2
                    o_, g_, x_, mg_ = oTt[i], Gt[i], xr[i], mg[i]
                    for dc in range(8):
                        ya = py[cnt["y"] % 4]
                        yb = py[(cnt["y"] + 1) % 4]
                        cnt["y"] += 2
                        for ec in range(4):
                            S.mm(ya.t[:], wosb.t[:, ec, dc * 128:(dc + 1) * 128], o_.t[:, ec, :],
                                 ec == 0, ec == 3, reads=[wosb, o_], pwrites=[ya])
                        for ec in range(4):
                            S.mm(yb.t[:], wodf.t[:, ec, dc * 128:(dc + 1) * 128], o_.t[:, 4 + ec, :],
                                 ec == 0, ec == 3, reads=[wodf, o_], pwrites=[yb])
                        k = cnt["m"] % 2
                        cnt["m"] += 1
                        S.tt("dve", m1[k].t[:], ya.t[:], g_.t[:, dc, :], ALU.mult,
                             reads=[ya, g_], writes=[m1[k]])
                        S.tt("dve", m2[k].t[:], yb.t[:], g_.t[:, 8 + dc, :], ALU.mult,
                             reads=[yb, g_], writes=[m2[k]])
                        S.tt("pool", mg_.t[:, dc, :], m1[k].t[:], m2[k].t[:], ALU.add,
                             reads=[m1[k], m2[k]], pwrites=[mg_])
                    for sub in range(4):
                        k = cnt["s"] % 2
                        cnt["s"] += 1
                        pa = po[cnt["o"] % 4]
                        pb = po[(cnt["o"] + 1) % 4]
                        cnt["o"] += 2
                        for half, p_ in ((0, pa), (1, pb)):
                            for dc in range(8):
                                S.mm(p_.t[:], mg_.t[:, dc, sub * 128:(sub + 1) * 128],
                                     wout.t[:, dc, half * 512:(half + 1) * 512],
                                     dc == 0, dc == 7, reads=[mg_, wout], pwrites=[p_])
                        S.act(junk.t[:], pa.t[:], AF.Square, accum_out=ssa[k].t[:, 0:1],
                              reads=[pa], writes=[junk], pwrites=[ssa[k]])
                        S.act(junk.t[:], pb.t[:], AF.Square, accum_out=ssa[k].t[:, 1:2],
                              reads=[pb], writes=[junk], pwrites=[ssa[k]])
                        S.tt("dve", sst[k].t[:], ssa[k].t[:, 0:1], ssa[k].t[:, 1:2], ALU.add,
                             reads=[ssa[k]], writes=[sst[k]])
                        S.act(lnv[k].t[:], sst[k].t[:], AF.Ln, bias=EPS, scale=1.0 / D,
                              reads=[sst[k]], writes=[lnv[k]])
                        S.act(rstd[k].t[:], lnv[k].t[:], AF.Exp, scale=-0.5, reads=[lnv[k]],
                              writes=[rstd[k]])
                        for half, p_ in ((0, pa), (1, pb)):
                            hs = slice(half * 512, (half + 1) * 512)
                            S.stt(t1[k].t[:, hs], p_.t[:], rstd[k].t[:], pg.t[:, hs], ALU.mult, ALU.mult,
                                  reads=[p_, rstd[k], pg], pwrites=[t1[k]])
                        S.tt("pool", ot[k].t[:], t1[k].t[:], x_.t[:, sub, :], ALU.add,
                             reads=[t1[k], x_], writes=[ot[k]])
                        r0 = j * 512 + sub * 128
                        S.dma(out_d[r0:r0 + 128, :], ot[k].t[:], reads=[ot[k]])

                load(0)
                for j in range(NSLOT):
                    if j + 1 < NSLOT:
                        load(j + 1)
                    tile(j)
                S.emit()

        if "C" in phases:
            phase_C()
    return nc


def make_consts(SEQ, parity):
    SQ = SEQ // 2
    NSLOT = SQ // 512
    bf = ml_dtypes.bfloat16
    ident = np.eye(128, dtype=np.float32).astype(bf)
    jj = np.arange(128)[:, None]
    s_ = np.arange(128)[None, :]
    negtri = np.concatenate([-(jj >= s_).astype(np.float32), -np.ones((128, 128), np.float32)], axis=1).astype(bf)
    ones = np.ones((128, 128), np.float32).astype(bf)
    onesf = np.full((128, 128), 1.0 / 128, np.float32)
    xx = np.arange(STRIP_W)[None, :]
    ss = np.arange(128)[:, None]
    tri = ((xx - 384) > ss).astype(np.float32)
    zer = np.zeros((128, STRIP_W), np.float32)
    one = np.ones((128, STRIP_W), np.float32)
    lo_set = [tri, zer]
    hi_set = [one, tri]
    sets = [lo_set, hi_set] if parity == 0 else [hi_set, lo_set]
    strips = np.concatenate([sets[0][0], sets[0][1], sets[1][0], sets[1][1]], axis=1).astype(bf)
    bidx, nb = bias_index(NSLOT)
    tiles = tiles_for(parity, NSLOT)
    bias = np.zeros((128, nb), np.float32)
    sl = np.arange(128, dtype=np.float64)
    for (h, j, kb), c in bidx.items():
        tref = 512 * tiles[j] + 256
        bias[:, c] = np.exp(SLOPES[h] * (128 * kb + sl - tref)).astype(np.float32)
        if kb // 4 > tiles[j]:
            bias[:, c] = 0.0
    return dict(ident=ident, negtri=negtri, ones=ones, onesf=onesf, strips=strips, biastab=bias)


def make_in_maps(SEQ, x, pre_norm_g, w_in, b_gate, lambda_q1, lambda_k1, lambda_q2, lambda_k2,
                 subln_g, w_o_sb, w_o_diff, w_out, post_norm_g):
    f = lambda a: np.ascontiguousarray(np.asarray(a, dtype=np.float32))
    x = f(x)
    SQ = SEQ // 2
    NSLOT = SQ // 512
    shared = dict(
        w_in=f(w_in[0]), w_o_sb=f(w_o_sb[0]), w_o_df=f(w_o_diff[0]), w_out=f(w_out[0]),
        pre_g=f(np.broadcast_to(np.asarray(pre_norm_g[0])[None, :], (128, D))),
        post_g=f(np.broadcast_to(np.asarray(post_norm_g[0])[None, :], (128, D))),
        bgate=f(np.asarray(b_gate[0]).reshape(16, 128).T),
        subln=f(np.asarray(subln_g[0]).reshape(128, 1)),
        lam_in=f(np.broadcast_to(np.concatenate([np.asarray(lambda_q1[0]), np.asarray(lambda_k1[0]),
                                                 np.asarray(lambda_q2[0]), np.asarray(lambda_k2[0])])[None, :],
                                 (128, 4 * HD))),
    )
    consts = [make_consts(SEQ, 0), make_consts(SEQ, 1)]
    in_maps = []
    for c in range(NCORES):
        b, par = c // 2, c % 2
        tiles = tiles_for(par, NSLOT)
        xb = x[b]
        xq = np.concatenate([xb[t * 512:(t + 1) * 512] for t in tiles], axis=0)
        m = dict(shared)
        m.update(consts[par])
        m["xk"] = np.ascontiguousarray(xb)
        m["xq"] = np.ascontiguousarray(xq)
        in_maps.append(m)
    return in_maps


_NC_CACHE = {}


def kernel(x, pre_norm_g, w_in, b_gate, lambda_q1, lambda_k1, lambda_q2, lambda_k2,
           subln_g, w_o_sb, w_o_diff, w_out, post_norm_g):
    x = np.asarray(x)
    B, SEQ, _ = x.shape
    SQ = SEQ // 2
    NSLOT = SQ // 512
    in_maps = make_in_maps(SEQ, x, pre_norm_g, w_in, b_gate, lambda_q1, lambda_k1, lambda_q2,
                           lambda_k2, subln_g, w_o_sb, w_o_diff, w_out, post_norm_g)
    if SEQ not in _NC_CACHE:
        _NC_CACHE[SEQ] = build_program(SEQ)
    nc = _NC_CACHE[SEQ]
    res = run_bass_kernel_spmd(nc, in_maps, core_ids=list(range(NCORES)))
    out = np.empty((B, SEQ, D), np.float32)
    for c in range(NCORES):
        b, par = c // 2, c % 2
        tiles = tiles_for(par, NSLOT)
        o = np.asarray(res.results[c]["out"]).reshape(SQ, D)
        for j, t in enumerate(tiles):
            out[b, t * 512:(t + 1) * 512] = o[j * 512:(j + 1) * 512]
    return out
```
